# Optimizing a Trainium2 kernel written in Bass

```python
import math
import jax, jax.numpy as jnp
from jax import lax
import numpy as np

D_MODEL = 1024
BATCH = 16
SEQ = 2048
DEPTH = 1

CHUNK = 64
N_Q_HEADS = 8
N_KV_HEADS = 2
HEAD_DIM = 64
Q_REP = N_Q_HEADS // N_KV_HEADS
ATTN_WIDTH = N_Q_HEADS * HEAD_DIM
KV_WIDTH = N_KV_HEADS * HEAD_DIM
WINDOW = 128
WINDOW_CHUNKS = WINDOW // CHUNK
BAND = (WINDOW_CHUNKS + 1) * CHUNK
ROPE_THETA = 500000.0
ROT_DIM = HEAD_DIM // 4
SSM_WIDTH = D_MODEL - ATTN_WIDTH
SSM_GROUP = 16
N_SSM_GROUPS = SSM_WIDTH // SSM_GROUP
SSM_STATE = 64
MIX_WIDTH = ATTN_WIDTH + SSM_WIDTH
IN_WIDTH = ATTN_WIDTH + 2 * KV_WIDTH + SSM_WIDTH
D_FF = -(-8 * D_MODEL // (3 * 256)) * 256
PLE_DIM = 256
EPS = 1e-6
DT_MIN = 1e-3
DT_MAX = 1e-1
LAMBDA_RE_MAX = -1e-4
NEG_INF = -1e30

kernel_name = "hymba_swa_sink_s5_swiglu_ple"

F32 = jnp.float32


def rms_norm(t, g):
    tf = t.astype(F32)
    y = tf * lax.rsqrt(jnp.mean(tf * tf, axis=-1, keepdims=True) + EPS)
    return (y * g.astype(F32)).astype(t.dtype)


def partial_rotary(t, positions):
    half = ROT_DIM // 2
    inv_freq = ROPE_THETA ** (-jnp.arange(half, dtype=F32) * (2.0 / ROT_DIM))
    ang = positions.astype(F32)[..., None] * inv_freq
    cos = jnp.cos(ang)[:, :, None, :]
    sin = jnp.sin(ang)[:, :, None, :]
    tr = t[..., :ROT_DIM].astype(F32)
    t1, t2 = tr[..., :half], tr[..., half:]
    rot = jnp.concatenate([t1 * cos - t2 * sin, t1 * sin + t2 * cos], axis=-1).astype(t.dtype)
    return jnp.concatenate([rot, t[..., ROT_DIM:]], axis=-1)


def sliding_window_attention(q, k, v, sinks):
    B, L = q.shape[0], q.shape[1]
    nc = L // CHUNK
    qc = q.reshape(B, nc, CHUNK, N_KV_HEADS, Q_REP, HEAD_DIM)

    def band(t):
        tc = t.reshape(B, nc, CHUNK, N_KV_HEADS, HEAD_DIM)
        tp = jnp.pad(tc, ((0, 0), (WINDOW_CHUNKS, 0), (0, 0), (0, 0), (0, 0)))
        return jnp.concatenate([tp[:, i:i + nc] for i in range(WINDOW_CHUNKS + 1)], axis=2)

    kb = band(k)
    vb = band(v)
    scores = jnp.einsum('bcqhgd,bcjhd->bchgqj', qc, kb,
                        preferred_element_type=F32) * (HEAD_DIM ** -0.5)
    key_chunk = (jnp.arange(nc)[:, None] - WINDOW_CHUNKS
                 + jnp.arange(BAND)[None, :] // CHUNK)
    valid = (key_chunk >= 0)[None, :, None, None, None, :]
    scores = jnp.where(valid, scores, NEG_INF)
    sink = sinks.astype(F32).reshape(N_KV_HEADS, Q_REP)[None, None, :, :, None, None]
    m = jnp.maximum(jnp.max(scores, axis=-1, keepdims=True), sink)
    pr = jnp.exp(scores - m)
    denom = jnp.sum(pr, axis=-1, keepdims=True) + jnp.exp(sink - m)
    w = (pr / denom).astype(v.dtype)
    out = jnp.einsum('bchgqj,bcjhd->bcqhgd', w, vb)
    return out.reshape(B, L, ATTN_WIDTH)


def s5_ssm(u, lam_re, lam_im, b_re, b_im, c_re, c_im, d_skip, log_dt, glu_w, glu_b):
    Bsz, L = u.shape[0], u.shape[1]
    uf = u.astype(F32).reshape(Bsz, L, N_SSM_GROUPS, SSM_GROUP)
    lr = jnp.minimum(lam_re.astype(F32), LAMBDA_RE_MAX)
    li = lam_im.astype(F32)
    dt = jnp.exp(log_dt.astype(F32))[:, None]
    mag = jnp.exp(lr * dt)
    ab_re = mag * jnp.cos(li * dt)
    ab_im = mag * jnp.sin(li * dt)
    nr = ab_re - 1.0
    ni = ab_im
    den = lr * lr + li * li
    f_re = (nr * lr + ni * li) / den
    f_im = (ni * lr - nr * li) / den
    br = b_re.astype(F32)
    bi = b_im.astype(F32)
    bb_re = f_re[..., None] * br - f_im[..., None] * bi
    bb_im = f_re[..., None] * bi + f_im[..., None] * br
    bu_re = jnp.einsum('blgc,gnc->blgn', uf, bb_re)
    bu_im = jnp.einsum('blgc,gnc->blgn', uf, bb_im)
    a_re = jnp.broadcast_to(ab_re, (1, L, N_SSM_GROUPS, SSM_STATE))
    a_im = jnp.broadcast_to(ab_im, (1, L, N_SSM_GROUPS, SSM_STATE))

    def combine(e1, e2):
        a1r, a1i, b1r, b1i = e1
        a2r, a2i, b2r, b2i = e2
        return (a2r * a1r - a2i * a1i,
                a2r * a1i + a2i * a1r,
                a2r * b1r - a2i * b1i + b2r,
                a2r * b1i + a2i * b1r + b2i)

    _, _, s_re, s_im = lax.associative_scan(combine, (a_re, a_im, bu_re, bu_im), axis=1)
    y = (jnp.einsum('blgn,gcn->blgc', s_re, c_re.astype(F32))
         - jnp.einsum('blgn,gcn->blgc', s_im, c_im.astype(F32))
         + d_skip.astype(F32) * uf)
    y = jax.nn.gelu(y.reshape(Bsz, L, SSM_WIDTH)).astype(u.dtype)
    return y * jax.nn.sigmoid(y @ glu_w + glu_b)


def setup_inputs(seed: int = 0) -> dict:
    key = jax.random.key(seed)
    ks = jax.random.split(key, 32)
    nrm = jax.random.normal
    x = nrm(ks[0], (BATCH, SEQ, D_MODEL), F32)
    p = nrm(ks[1], (DEPTH, BATCH, SEQ, PLE_DIM), F32)
    offset = jax.random.randint(ks[2], (BATCH, 1), 0, 8192)
    positions = (offset + jnp.arange(SEQ)[None, :]).astype(jnp.int32)
    G, N = N_SSM_GROUPS, SSM_STATE
    return {
        "x": x,
        "p": p,
        "positions": positions,
        "norm1_g": 1.0 + 0.02 * nrm(ks[3], (DEPTH, D_MODEL), F32),
        "w_in": nrm(ks[4], (DEPTH, D_MODEL, IN_WIDTH), F32) * D_MODEL ** -0.5,
        "q_norm_g": 1.0 + 0.02 * nrm(ks[5], (DEPTH, HEAD_DIM), F32),
        "k_norm_g": 1.0 + 0.02 * nrm(ks[6], (DEPTH, HEAD_DIM), F32),
        "attn_sinks": 0.5 * nrm(ks[7], (DEPTH, N_Q_HEADS), F32),
        "ssm_lambda_re": -0.5 + 0.01 * nrm(ks[8], (DEPTH, G, N), F32),
        "ssm_lambda_im": math.pi * jnp.arange(N, dtype=F32)[None, None, :]
                         + 0.01 * nrm(ks[9], (DEPTH, G, N), F32),
        "ssm_b_re": nrm(ks[10], (DEPTH, G, N, SSM_GROUP), F32) * (2.0 * SSM_GROUP) ** -0.5,
        "ssm_b_im": nrm(ks[11], (DEPTH, G, N, SSM_GROUP), F32) * (2.0 * SSM_GROUP) ** -0.5,
        "ssm_c_re": nrm(ks[12], (DEPTH, G, SSM_GROUP, N), F32) * (2.0 * N) ** -0.5,
        "ssm_c_im": nrm(ks[13], (DEPTH, G, SSM_GROUP, N), F32) * (2.0 * N) ** -0.5,
        "ssm_d": nrm(ks[14], (DEPTH, G, SSM_GROUP), F32),
        "ssm_log_dt": jax.random.uniform(ks[15], (DEPTH, G), F32,
                                         minval=math.log(DT_MIN), maxval=math.log(DT_MAX)),
        "glu_w": nrm(ks[16], (DEPTH, SSM_WIDTH, SSM_WIDTH), F32) * SSM_WIDTH ** -0.5,
        "glu_b": 0.01 * nrm(ks[17], (DEPTH, SSM_WIDTH), F32),
        "attn_out_norm_g": 1.0 + 0.02 * nrm(ks[18], (DEPTH, ATTN_WIDTH), F32),
        "ssm_out_norm_g": 1.0 + 0.02 * nrm(ks[19], (DEPTH, SSM_WIDTH), F32),
        "w_out": nrm(ks[20], (DEPTH, MIX_WIDTH, D_MODEL), F32) * MIX_WIDTH ** -0.5,
        "norm2_g": 1.0 + 0.02 * nrm(ks[21], (DEPTH, D_MODEL), F32),
        "w_ffn_gate": nrm(ks[22], (DEPTH, D_MODEL, D_FF), F32) * D_MODEL ** -0.5,
        "w_ffn_up": nrm(ks[23], (DEPTH, D_MODEL, D_FF), F32) * D_MODEL ** -0.5,
        "w_ffn_down": nrm(ks[24], (DEPTH, D_FF, D_MODEL), F32) * D_FF ** -0.5,
        "ple_norm_g": 1.0 + 0.02 * nrm(ks[25], (DEPTH, D_MODEL), F32),
        "w_ple_gate": nrm(ks[26], (DEPTH, D_MODEL, D_MODEL), F32) * D_MODEL ** -0.5,
        "b_ple_gate": 0.01 * nrm(ks[27], (DEPTH, D_MODEL), F32),
        "w_ple_proj": nrm(ks[28], (DEPTH, PLE_DIM, D_MODEL), F32) * PLE_DIM ** -0.5,
    }


def reference(x, p, positions, norm1_g, w_in, q_norm_g, k_norm_g, attn_sinks,
              ssm_lambda_re, ssm_lambda_im, ssm_b_re, ssm_b_im, ssm_c_re, ssm_c_im,
              ssm_d, ssm_log_dt, glu_w, glu_b, attn_out_norm_g, ssm_out_norm_g, w_out,
              norm2_g, w_ffn_gate, w_ffn_up, w_ffn_down, ple_norm_g, w_ple_gate,
              b_ple_gate, w_ple_proj):
    B, L = x.shape[0], x.shape[1]
    h = x
    for i in range(DEPTH):
        hn = rms_norm(h, norm1_g[i])
        z = hn @ w_in[i]
        q = z[..., :ATTN_WIDTH].reshape(B, L, N_Q_HEADS, HEAD_DIM)
        k = z[..., ATTN_WIDTH:ATTN_WIDTH + KV_WIDTH].reshape(B, L, N_KV_HEADS, HEAD_DIM)
        v = z[..., ATTN_WIDTH + KV_WIDTH:ATTN_WIDTH + 2 * KV_WIDTH].reshape(B, L, N_KV_HEADS, HEAD_DIM)
        u = z[..., ATTN_WIDTH + 2 * KV_WIDTH:]
        q = partial_rotary(rms_norm(q, q_norm_g[i]), positions)
        k = partial_rotary(rms_norm(k, k_norm_g[i]), positions)
        a_out = sliding_window_attention(q, k, v, attn_sinks[i])
        s_out = s5_ssm(u, ssm_lambda_re[i], ssm_lambda_im[i], ssm_b_re[i], ssm_b_im[i],
                       ssm_c_re[i], ssm_c_im[i], ssm_d[i], ssm_log_dt[i], glu_w[i], glu_b[i])
        mix = jnp.concatenate([rms_norm(a_out, attn_out_norm_g[i]),
                               rms_norm(s_out, ssm_out_norm_g[i])], axis=-1)
        h = h + mix @ w_out[i]
        hn2 = rms_norm(h, norm2_g[i])
        h = h + (jax.nn.silu(hn2 @ w_ffn_gate[i]) * (hn2 @ w_ffn_up[i])) @ w_ffn_down[i]
        gate = jax.nn.sigmoid(rms_norm(h, ple_norm_g[i]) @ w_ple_gate[i] + b_ple_gate[i])
        h = h + gate * (p[i] @ w_ple_proj[i])
    return h
```

```python
import math
import os
from contextlib import ExitStack

import numpy as np
import concourse.bass as bass
import concourse.mybir as mybir
from concourse.bass_utils import run_bass_kernel_spmd

F32 = mybir.dt.float32
BF16 = mybir.dt.bfloat16
I32 = mybir.dt.int32
AF = mybir.ActivationFunctionType
ALU = mybir.AluOpType

NCORE = 8
D = 1024
L = 2048
TB = 1024
NTT = TB // 128
DFF = 2816
NFF = DFF // 128
EPS = 1e-6
TWO_PI = 2.0 * math.pi
C1 = 6.28125
C2 = TWO_PI - C1
PI_CL = 3.1415925
NSLOT = 12
FFG = [(0, 8), (8, 16), (16, 22)]
SCAN_ENG = "pool"
LAG = 3
TR_MM = os.environ.get("K_TR_MM", "1") == "1"
QK_ENG = "dve"


class Res:
    __slots__ = ("w", "r")

    def __init__(self):
        self.w = None
        self.r = []


def RL(n):
    return [Res() for _ in range(n)]


class Fw:
    def __init__(self, nc, stack):
        self.nc = nc
        self.stack = stack
        self.eng = {"pe": nc.tensor, "act": nc.scalar, "dve": nc.vector, "pool": nc.gpsimd, "sp": nc.sync}
        self.sem = {k: stack.enter_context(nc.semaphore("s_" + k)) for k in self.eng}
        self.cnt = {k: 0 for k in self.eng}
        self.seen = {k: {} for k in self.eng}
        self.pend = {k: ([], []) for k in self.eng}
        self.dsem = {}
        self.dcnt = {}
        self.rec = None

    def _wait(self, e, deps):
        for ev in deps:
            if ev is None:
                continue
            sem, val, key = ev
            if key == "pe" and e == "pe":
                continue
            if self.seen[e].get(key, 0) >= val:
                continue
            self.seen[e][key] = val
            self.eng[e].wait_ge(sem, val)

    @staticmethod
    def _deps(reads, writes):
        d = []
        for r in reads:
            d.append(r.w)
        for w in writes:
            d.append(w.w)
            d.extend(w.r)
        return d

    def op(self, e, fn, reads=(), writes=(), signal=True):
        if self.rec is not None:
            self.rec.append((e, fn, list(reads), list(writes), signal))
            return None
        self._wait(e, self._deps(reads, writes))
        inst = fn(self.eng[e])
        pr, pw = self.pend[e]
        pr.extend(reads)
        pw.extend(writes)
        if signal:
            self.cnt[e] += 1
            inst.then_inc(self.sem[e], 1)
            ev = (self.sem[e], self.cnt[e], e)
            for r in pr:
                r.r.append(ev)
            for w in pw:
                w.w = ev
                w.r = []
            self.pend[e] = ([], [])
            return ev
        return None

    def dma(self, e, out, in_, reads=(), writes=(), slot="d0", **kw):
        if slot not in self.dsem:
            self.dsem[slot] = self.stack.enter_context(self.nc.semaphore("q_" + slot))
            self.dcnt[slot] = 0
        self._wait(e, self._deps(reads, writes))
        inst = self.eng[e].dma_start(out=out, in_=in_, **kw)
        self.dcnt[slot] += 16
        inst.then_inc(self.dsem[slot], 16)
        ev = (self.dsem[slot], self.dcnt[slot], "dma_" + slot)
        for r in reads:
            r.r.append(ev)
        for w in writes:
            w.w = ev
            w.r = []
        return ev


def make_consts():
    c = np.zeros((128, 648), np.float32)
    c[:, 0:128] = np.eye(128, dtype=np.float32)
    c[:, 128:256] = 1.0
    for hf in range(2):
        c[hf * 64:(hf + 1) * 64, 256 + hf * 64:256 + (hf + 1) * 64] = 1.0
    for hf in range(2):
        o = hf * 64
        for d in range(8):
            c[o + d + 8, 384 + o + d] = -1.0
            c[o + d, 384 + o + d + 8] = 1.0
    for i in range(4):
        c[i * 32:(i + 1) * 32, 512 + i * 32:512 + (i + 1) * 32] = 1.0
    for p in range(128):
        d = p % 64
        c[p, 640] = (500000.0 ** (-(d % 8) * (2.0 / 16.0))) if d < 16 else 0.0
        c[p, 641] = -0.5
    return c


def build_program():
    nc = bass.Bass("TRN2", target_bir_lowering=False)
    dram = lambda n, s, dt, k="ExternalInput": nc.dram_tensor(n, s, dt, kind=k).ap()
    x = dram("x", [2, L, D], F32)
    p_in = dram("p", [2, L, 256], F32)
    pos = dram("positions", [2, L], I32)
    cst = dram("cst", [128, 648], F32)
    norm1_g = dram("norm1_g", [1, D], F32)
    w_in = dram("w_in", [1, D, 1280], F32)
    q_norm_g = dram("q_norm_g", [1, 64], F32)
    k_norm_g = dram("k_norm_g", [1, 64], F32)
    attn_sinks = dram("attn_sinks", [1, 8], F32)
    lam_re = dram("ssm_lambda_re", [1, 32, 64], F32)
    lam_im = dram("ssm_lambda_im", [1, 32, 64], F32)
    b_re = dram("ssm_b_re", [1, 32, 64, 16], F32)
    b_im = dram("ssm_b_im", [1, 32, 64, 16], F32)
    c_re = dram("ssm_c_re", [1, 32, 16, 64], F32)
    c_im = dram("ssm_c_im", [1, 32, 16, 64], F32)
    ssm_d = dram("ssm_d", [1, 32, 16], F32)
    log_dt = dram("ssm_log_dt", [1, 32], F32)
    glu_w = dram("glu_w", [1, 512, 512], F32)
    glu_b = dram("glu_b", [1, 512], F32)
    ao_g = dram("attn_out_norm_g", [1, 512], F32)
    so_g = dram("ssm_out_norm_g", [1, 512], F32)
    w_out = dram("w_out", [1, D, D], F32)
    norm2_g = dram("norm2_g", [1, D], F32)
    w_gate = dram("w_ffn_gate", [1, D, DFF], F32)
    w_up = dram("w_ffn_up", [1, D, DFF], F32)
    w_down = dram("w_ffn_down", [1, DFF, D], F32)
    ple_g = dram("ple_norm_g", [1, D], F32)
    w_pg = dram("w_ple_gate", [1, D, D], F32)
    b_pg = dram("b_ple_gate", [1, D], F32)
    w_pp = dram("w_ple_proj", [1, 256, D], F32)
    out = dram("out", [2, L, D], F32, "ExternalOutput")

    with ExitStack() as st:
        fw = Fw(nc, st)
        st.enter_context(nc.allow_non_contiguous_dma(reason="small one-time parameter layout loads"))
        st.enter_context(nc.allow_low_precision(reason="bf16 matmul operands, fp32 accumulation"))
        sb = lambda n, s, dt: st.enter_context(nc.sbuf_tensor(n, s, dt))
        h = sb("h", [128, NTT, D], F32)
        xT = sb("xT", [128, 8, TB], BF16)
        un = sb("un", [128, 8, TB], BF16)
        kdT = sb("kdT", [128, 2, 128 + TB], BF16)
        vtm = sb("vtm", [128, NTT + 1, 128], BF16)
        Vs = sb("Vs", [128, 32, 129], F32)
        Sb = sb("Sb", [128, 32, 128], BF16)
        cosT = sb("cosT", [128, TB], BF16)
        sinT = sb("sinT", [128, TB], BF16)
        rk = sb("rk", [128, NTT + 1, 2], F32)
        ring = sb("ring", [128, NSLOT, 1024], BF16)
        W_V = sb("W_V", [128, 4, 2, 8, 128], BF16)
        W_K = sb("W_K", [128, 4, 8, 128], BF16)
        W_S = sb("W_S", [128, 2, 16, 8, 32], BF16)
        cstf = sb("cstf", [128, 648], F32)
        cstb = sb("cstb", [128, 640], BF16)
        xs = sb("xs", [128, 2, 1024], BF16)
        ftmp = sb("ftmp", [128, 6, 512], F32)
        btmp = sb("btmp", [128, 10, 512], BF16)
        small = sb("small", [128, 80], F32)
        gains = sb("gains", [128, 48], F32)
        bpleb = sb("bpleb", [1, 1024], BF16)
        pstage = sb("pstage", [128, 2, 256], F32)
        psb = [st.enter_context(nc.psum_tensor("ps%d" % i, [128, 512], F32)) for i in range(8)]

        identb = cstb[:, 0:128]
        onesb = cstb[:, 128:256]
        BOb = cstb[:, 256:384]
        PiTb = cstb[:, 384:512]
        identf = cstf[:, 0:128]
        BDf = cstf[:, 512:640]
        invf = cstf[:, 640:641]
        mhalf = cstf[:, 641:642]
        pT = Sb[:].rearrange("p a b -> p (a b)")[:, 0:2048].rearrange("p (a b) -> p a b", a=2)

        R_h = RL(NTT)
        R_xT = [RL(NTT) for _ in range(8)]
        R_un = [RL(2) for _ in range(8)]
        R_kd = RL(3)
        R_v = RL(NTT + 1)
        R_Vs = RL(1)[0]
        R_Sb = RL(1)[0]
        R_cs = RL(1)[0]
        R_ring = RL(NSLOT)
        R_ps = RL(8)
        R_xs = RL(2)
        R_sqs = RL(1)[0]
        R_ft = RL(6)
        R_bt = RL(10)
        R_rk = RL(NTT + 1)
        R_small = RL(1)[0]
        R_scan = RL(1)[0]
        R_pst = RL(2)
        R_setup = RL(1)[0]
        R_ssmw = RL(1)[0]
        R_out = RL(1)[0]

        st.enter_context(nc.Block())

        bank_ctr = [0]

        def bank():
            i = bank_ctr[0] % 8
            bank_ctr[0] += 1
            return psb[i], R_ps[i]

        def tr(e, o, i_):
            if TR_MM:
                return e.matmul(o, lhsT=i_, rhs=identb, start=True, stop=True, is_transpose=True)
            return e.transpose(out=o, in_=i_, identity=identb)

        def mm(out, lhsT, rhs, start, stop, reads, writes, signal=None, tp=None):
            kw = {}
            if tp is not None:
                kw["tile_position"] = tp
            return fw.op("pe", lambda e: e.matmul(out, lhsT=lhsT, rhs=rhs, start=start, stop=stop, **kw),
                         reads=reads, writes=writes, signal=(stop if signal is None else signal))

        def act(out, in_, func, reads, writes, scale=1.0, bias=0.0, accum_out=None):
            kw = {}
            if accum_out is not None:
                kw["accum_out"] = accum_out
            return fw.op("act", lambda e: e.activation(out=out, in_=in_, func=func, scale=scale, bias=bias, **kw),
                         reads=reads, writes=writes)

        def tt(out, in0, in1, op, reads, writes, e="dve"):
            return fw.op(e, lambda g: g.tensor_tensor(out=out, in0=in0, in1=in1, op=op), reads=reads, writes=writes)

        def ts(out, in0, s1, s2, op0, op1, reads, writes, e="dve"):
            if op1 is None:
                return fw.op(e, lambda g: g.tensor_scalar(out=out, in0=in0, scalar1=s1, scalar2=None, op0=op0),
                             reads=reads, writes=writes)
            return fw.op(e, lambda g: g.tensor_scalar(out=out, in0=in0, scalar1=s1, scalar2=s2, op0=op0, op1=op1),
                         reads=reads, writes=writes)

        def stt(out, in0, scalar, in1, op0, op1, reads, writes):
            return fw.op("dve", lambda g: g.scalar_tensor_tensor(out=out, in0=in0, scalar=scalar, in1=in1, op0=op0, op1=op1),
                         reads=reads, writes=writes)

        def cp(out, in_, reads, writes, e="dve"):
            if e == "act":
                return fw.op("act", lambda g: g.copy(out=out, in_=in_), reads=reads, writes=writes)
            return fw.op(e, lambda g: g.tensor_copy(out=out, in_=in_), reads=reads, writes=writes)

        def memset(ap, val, writes, e="dve"):
            return fw.op(e, lambda g: g.memset(ap, val), writes=writes)

        def recip(out, in_, reads, writes):
            return fw.op("dve", lambda g: g.reciprocal(out=out, in_=in_), reads=reads, writes=writes)

        sslot = ["setup"]

        def sdma(out_, in_):
            fw.dma("sp", out_, in_, slot=sslot[0])

        sdma(cstf[:], cst)
        sdma(gains[:, 0:8], norm1_g[0].rearrange("(k p) -> p k", p=128))
        sdma(gains[:, 8:16], norm2_g[0].rearrange("(k p) -> p k", p=128))
        sdma(gains[:, 16:24], ple_g[0].rearrange("(k p) -> p k", p=128))
        for hf in range(2):
            sdma(gains[hf * 64:(hf + 1) * 64, 24:25], q_norm_g[0].rearrange("(d o) -> d o", o=1))
            sdma(gains[hf * 64:(hf + 1) * 64, 25:26], k_norm_g[0].rearrange("(d o) -> d o", o=1))
            sdma(gains[hf * 64:(hf + 1) * 64, 38:42],
                 attn_sinks[0:1, :].rearrange("o (i f) -> o f i", f=2)[:, hf, :].partition_broadcast(64))
        sdma(gains[:, 26:30], ao_g[0].rearrange("(k p) -> p k", p=128))
        sdma(gains[:, 30:34], so_g[0].rearrange("(k p) -> p k", p=128))
        sdma(gains[:, 34:38], glu_b[0].rearrange("(k p) -> p k", p=128))
        sdma(gains[:, 42:46], ssm_d[0].rearrange("(s g) c -> (g c) s", s=4))
        R_bple = RL(1)[0]
        bplef = ftmp[0:1, 0:2, :].rearrange("p a b -> p (a b)")
        sdma(bplef, b_pg[0:1, :])
        ev_setup = (fw.dsem["setup"], fw.dcnt["setup"], "dma_setup")
        R_setup.w = ev_setup
        for t in range(NTT):
            fw.dma("sp", h[:, t, :], x[0, t * 128:(t + 1) * 128, :], writes=[R_h[t]], slot="x%d" % t)
        sslot[0] = "setup2"
        prm = sb("prm", [128, 3, 16], F32)
        sdma(prm[:, 0, :], lam_re[0].rearrange("(pr gp) n -> (gp n) pr", gp=2))
        sdma(prm[:, 1, :], lam_im[0].rearrange("(pr gp) n -> (gp n) pr", gp=2))
        for gp in range(2):
            sdma(prm[gp * 64:(gp + 1) * 64, 2, :],
                 log_dt[0:1, :].rearrange("o (pr gp) -> o gp pr", gp=2)[:, gp, :].partition_broadcast(64))
        v4 = lambda a: a.rearrange("p (a b c) -> p a b c", a=2, b=16)
        Sbf = Sb[:].rearrange("p a b -> p (a b)").bitcast(F32)
        Bm = v4(Sbf[:, 0:1024])
        Cm = v4(Sbf[:, 1024:2048])
        W_Sf = W_S[:].rearrange("p a b c d -> p (a b c d)").bitcast(F32)
        W_Kf = W_K[:].rearrange("p a b c -> p (a b c)").bitcast(F32)
        Xm = W_Sf[:, 0:1024].rearrange("p (a s n) -> p a s n", a=2, s=4)
        memset(Bm, 0.0, [R_ssmw])
        memset(W_Sf[:, 0:1024], 0.0, [R_ssmw])
        ev_ms = R_ssmw.w
        fw._wait("sp", [ev_ms])
        for part, (bsrc, csrc) in enumerate([(b_re, c_re), (b_im, c_im)]):
            bs = bsrc[0].rearrange("(pr gp) n c -> gp n pr c", gp=2)
            for gp in range(2):
                sdma(Bm[gp * 64:(gp + 1) * 64, part, :, gp * 16:(gp + 1) * 16], bs[gp])
            for g in range(32):
                s_, i_, gp = g // 8, (g % 8) // 2, g % 2
                sdma(Xm[32 * i_ + 16 * gp:32 * i_ + 16 * gp + 16, part, s_, 64 * gp:64 * gp + 64], csrc[0, g])
        R_ssmw.w = (fw.dsem["setup2"], fw.dcnt["setup2"], "dma_setup2")
        S = [R_setup]

        cp(cstb[:], cstf[:, 0:640], S, [R_setup])
        cp(bpleb[:], bplef, S, [R_bple, R_ft[0], R_ft[1]], e="act")
        act(gains[:, 38:42], gains[:, 38:42], AF.Exp, S, [R_setup])
        es2 = gains[:, 38:42]
        esb = sb("esb", [128, 4, 64], BF16)
        cp(esb[:], es2.unsqueeze(2).to_broadcast([128, 4, 64]), S, [R_setup])

        def range_reduce(dst, src, n, shift, rs):
            t = ftmp[:, 4, 0:n]
            ki = ftmp[:, 5, 0:n].bitcast(I32)
            ts(t, src, 1.0, shift, ALU.mult, ALU.add, rs, [R_ft[4]])
            ts(dst, t, 1.0 / TWO_PI, None, ALU.mult, None, [R_ft[4]], rs)
            cp(ki, dst, rs, [R_ft[5]])
            cp(dst, ki, [R_ft[5]], rs)
            stt(t, dst, -C1, t, ALU.mult, ALU.add, rs, [R_ft[4]])
            stt(t, dst, -C2, t, ALU.mult, ALU.add, rs, [R_ft[4]])
            ts(dst, t, -PI_CL, PI_CL, ALU.max, ALU.min, [R_ft[4]], rs)

        W = [R_ssmw]
        ssm = sb("ssmt", [128, 8, 144], F32)
        scanA = sb("scanA", [128, 3, 32], F32)
        TA = sb("TA", [128, 2, 32, 16], F32)
        Cc = sb("Cc", [128, 32, 8], F32)

        def ssm_precompute():
            for part in range(2):
                pb, rpb = bank()
                for s_ in range(4):
                    fw.op("pe", lambda e, o=pb[:, s_ * 128:(s_ + 1) * 128], i_=Xm[:, part, s_, :]:
                          e.matmul(o, lhsT=i_, rhs=identf, start=True, stop=True, is_transpose=True),
                          reads=[R_ssmw] + S, writes=[rpb], signal=(s_ == 3))
                cp(Cm[:, part], pb[:, :].rearrange("p (a b) -> p a b", b=32), [rpb], [R_ssmw])
            fw.rec = []
            lr = prm[:, 0, :]
            li = prm[:, 1, :]
            ts(lr, lr, -1e-4, None, ALU.min, None, W, W)
            act(prm[:, 2, :], prm[:, 2, :], AF.Exp, W, W)
            dtv = prm[:, 2, :]
            misc = ssm[:, 7, :]
            tt(misc[:, 0:16], lr, dtv, ALU.mult, W, W)
            tt(misc[:, 16:32], li, dtv, ALU.mult, W, W)
            for tau in range(9):
                ts(ssm[:, 0, tau * 16:(tau + 1) * 16], misc[:, 0:16], float(tau), None, ALU.mult, None, W, W)
                ts(ssm[:, 1, tau * 16:(tau + 1) * 16], misc[:, 16:32], float(tau), None, ALU.mult, None, W, W)
            act(ssm[:, 2, :], ssm[:, 0, :], AF.Exp, W, W)
            range_reduce(ssm[:, 3, :], ssm[:, 1, :], 144, 0.0, W)
            range_reduce(ssm[:, 4, :], ssm[:, 1, :], 144, math.pi / 2.0, W)
            act(ssm[:, 3, :], ssm[:, 3, :], AF.Sin, W, W)
            act(ssm[:, 4, :], ssm[:, 4, :], AF.Sin, W, W)
            tt(ssm[:, 5, :], ssm[:, 2, :], ssm[:, 4, :], ALU.mult, W, W)
            tt(ssm[:, 6, :], ssm[:, 2, :], ssm[:, 3, :], ALU.mult, W, W)
            Are = lambda tau: ssm[:, 5, tau * 16:(tau + 1) * 16]
            Aim = lambda tau: ssm[:, 6, tau * 16:(tau + 1) * 16]
            nr = misc[:, 32:48]
            ni = Aim(1)
            den = misc[:, 48:64]
            t1 = misc[:, 64:80]
            t2 = misc[:, 80:96]
            fre = misc[:, 96:112]
            fim = misc[:, 112:128]
            ts(nr, Are(1), -1.0, None, ALU.add, None, W, W)
            tt(den, lr, lr, ALU.mult, W, W)
            tt(t1, li, li, ALU.mult, W, W)
            tt(den, den, t1, ALU.add, W, W)
            recip(den, den, W, W)
            tt(t1, nr, lr, ALU.mult, W, W)
            tt(t2, ni, li, ALU.mult, W, W)
            tt(t1, t1, t2, ALU.add, W, W)
            tt(fre, t1, den, ALU.mult, W, W)
            tt(t1, ni, lr, ALU.mult, W, W)
            tt(t2, nr, li, ALU.mult, W, W)
            tt(t1, t1, t2, ALU.subtract, W, W)
            tt(fim, t1, den, ALU.mult, W, W)
            cp(scanA[:, 0, 0:16], Are(8), W, W)
            cp(scanA[:, 0, 16:32], Are(8), W, W)
            ts(scanA[:, 1, 0:16], Aim(8), -1.0, None, ALU.mult, None, W, W)
            cp(scanA[:, 1, 16:32], Aim(8), W, W)

            pw = ssm[:, 7, 128:144]
            pr_prev, pi_prev = Are(8), Aim(8)
            for j in range(16):
                if j > 0:
                    nr_ = ssm[:, 0, (j % 2) * 32:(j % 2) * 32 + 16]
                    ni_ = ssm[:, 0, (j % 2) * 32 + 16:(j % 2) * 32 + 32]
                    tt(nr_, pr_prev, Are(8), ALU.mult, W, W)
                    tt(pw, pi_prev, Aim(8), ALU.mult, W, W)
                    tt(nr_, nr_, pw, ALU.subtract, W, W)
                    tt(ni_, pr_prev, Aim(8), ALU.mult, W, W)
                    tt(pw, pi_prev, Are(8), ALU.mult, W, W)
                    tt(ni_, ni_, pw, ALU.add, W, W)
                    pr_prev, pi_prev = nr_, ni_
                cp(TA[:, 0, 0:16, j], pr_prev, W, W)
                cp(TA[:, 0, 16:32, j], pr_prev, W, W)
                ts(TA[:, 1, 0:16, j], pi_prev, -1.0, None, ALU.mult, None, W, W)
                cp(TA[:, 1, 16:32, j], pi_prev, W, W)

            bc = lambda a: a.unsqueeze(2).to_broadcast([128, 16, 32])
            Bb = v4(W_Kf[:, 0:1024])
            T1 = ftmp[:, 0, :].rearrange("p (a b) -> p a b", a=16)
            T2 = ftmp[:, 1, :].rearrange("p (a b) -> p a b", a=16)
            WT = W + [R_ft[0], R_ft[1]]
            tt(T1, Bm[:, 0], bc(fre), ALU.mult, WT, WT)
            tt(T2, Bm[:, 1], bc(fim), ALU.mult, WT, WT)
            tt(Bb[:, 0], T1, T2, ALU.subtract, WT, WT)
            tt(T1, Bm[:, 1], bc(fre), ALU.mult, WT, WT)
            tt(T2, Bm[:, 0], bc(fim), ALU.mult, WT, WT)
            tt(Bb[:, 1], T1, T2, ALU.add, WT, WT)
            Z = Vs[:].rearrange("p a b -> p (a b)").bitcast(BF16)[:, 0:8192].rearrange("p (a b c d) -> p a b c d", a=2, b=8, c=16)
            for tau in range(8):
                tt(T1, Bb[:, 0], bc(Are(tau)), ALU.mult, WT, WT)
                tt(T2, Bb[:, 1], bc(Aim(tau)), ALU.mult, WT, WT)
                tt(Z[:, 0, tau], T1, T2, ALU.subtract, WT, WT)
                tt(T1, Bb[:, 1], bc(Are(tau)), ALU.mult, WT, WT)
                tt(T2, Bb[:, 0], bc(Aim(tau)), ALU.mult, WT, WT)
                tt(Z[:, 1, tau], T1, T2, ALU.add, WT, WT)
            for tp_ in range(8):
                a_r = bc(Are(tp_ + 1))
                a_i = bc(Aim(tp_ + 1))
                tt(T1, Cm[:, 0], a_r, ALU.mult, WT, WT)
                tt(T2, Cm[:, 1], a_i, ALU.mult, WT, WT)
                tt(W_S[:, 0, :, tp_, :], T1, T2, ALU.subtract, WT, WT)
                tt(T1, Cm[:, 0], a_i, ALU.mult, WT, WT)
                tt(T2, Cm[:, 1], a_r, ALU.mult, WT, WT)
                stt(W_S[:, 1, :, tp_, :], T1, -1.0, T2, ALU.mult, ALU.subtract, WT, WT)
            Cmb = v4(pstage[:].rearrange("p a b -> p (a b)").bitcast(BF16))
            cp(Cmb[:, 0], Cm[:, 0], W, W)
            ts(Cmb[:, 1], Cm[:, 1], -1.0, None, ALU.mult, None, W, W)
            chain = fw.rec
            fw.rec = None

            def partC():
                for s in range(4):
                    for part in range(2):
                        pb, rpb = bank()
                        pbb = pb[:].bitcast(BF16)
                        for j in range(8):
                            zin = Z[:, part, 7 - j, 4 * s:4 * s + 4, :].rearrange("p a b -> p (a b)")
                            fw.op("pe", lambda e, o=pbb[:, j * 128:(j + 1) * 128], zi=zin: tr(e, o, zi),
                                  reads=W + S, writes=[rpb], signal=(j == 7))
                        cp(W_V[:, s, part].rearrange("p a b -> p (a b)"), pbb[:, 0:1024], [rpb], W)
                for s in range(4):
                    for half in range(2):
                        pb, rpb = bank()
                        for t4 in range(4):
                            tau = half * 4 + t4
                            o = pb[:, t4 * 128:(t4 + 1) * 128]
                            mm(o, Z[:, 0, tau, 4 * s:4 * s + 4, :].rearrange("p a b -> p (a b)"),
                               Cmb[:, 0, 4 * s:4 * s + 4, :].rearrange("p a b -> p (a b)"), True, False, W, [rpb], signal=False)
                            mm(o, Z[:, 1, tau, 4 * s:4 * s + 4, :].rearrange("p a b -> p (a b)"),
                               Cmb[:, 1, 4 * s:4 * s + 4, :].rearrange("p a b -> p (a b)"), False, True, W, [rpb], signal=(t4 == 3))
                        for t4 in range(4):
                            tau = half * 4 + t4
                            o = pb[:, t4 * 128:(t4 + 1) * 128]
                            if tau == 0:
                                tt(ftmp[:, 0, 0:128], o, BDf, ALU.mult, [rpb] + S, [R_ft[0]])
                                stt(W_K[:, s, 0, :], identf, gains[:, 42 + s:43 + s], ftmp[:, 0, 0:128], ALU.mult, ALU.add,
                                    [R_ft[0]] + S, W)
                            else:
                                tt(W_K[:, s, tau, :], o, BDf, ALU.mult, [rpb] + S, W)

                fence = [(fw.sem[k], fw.cnt[k], k) for k in ("pe", "act", "dve") if fw.cnt[k] > 0]
                for r_ in [R_Sb, R_Vs, R_pst[0], R_pst[1]]:
                    r_.r.extend(fence)

            return chain, partC

        slot_ctr = [0]

        def load_chunk(srcs):
            i = slot_ctr[0] % NSLOT
            slot_ctr[0] += 1
            for dst, src in srcs:
                fw.dma("pool", dst(ring[:, i, :]), src, writes=[R_ring[i]], slot="ring%d" % i)
            return i

        def chunk_cols(wmat, c0, ncol=128):
            return [(lambda r: r.rearrange("p (k c) -> p k c", k=8)[:, :, 0:ncol] if ncol == 128 else
                     r.rearrange("p (k c) -> p k c", k=8)[:, :, 0:ncol],
                     wmat[:, c0:c0 + ncol].rearrange("(k p) c -> p k c", p=128))]

        def chunk_rows(wmat, kt):
            return [(lambda r: r, wmat[kt * 128:(kt + 1) * 128, :])]

        R_nsm = RL(4)
        junk = btmp[:, 8:10, :].rearrange("p a b -> p (a b)")

        def norm_driver(gcol0, base_lag):
            def s0(tile):
                j4 = tile % 4
                ss = small[:, 2 * j4:2 * j4 + 1]
                rs = small[:, 2 * j4 + 1:2 * j4 + 2]
                act(junk, h[:, tile, :], AF.Square, [R_h[tile]], [R_bt[8], R_bt[9], R_nsm[j4]], accum_out=ss)
                ts(rs, ss, 1.0 / D, EPS, ALU.mult, ALU.add, [R_nsm[j4]], [R_nsm[j4]], e="pool")
                tt(rs, rs, mhalf, ALU.pow, [R_nsm[j4]] + S, [R_nsm[j4]], e="pool")

            def s1(tile):
                j, j4 = tile % 2, tile % 4
                rs = small[:, 2 * j4 + 1:2 * j4 + 2]
                act(xs[:, j, :], h[:, tile, :], AF.Copy, [R_h[tile], R_nsm[j4]], [R_xs[j]], scale=rs)

            def s2(tile):
                j = tile % 2
                pb, rpb = bank()
                pbb = pb[:].bitcast(BF16)
                for kt in range(8):
                    fw.op("pe", lambda e, o=pbb[:, kt * 128:(kt + 1) * 128], i_=xs[:, j, kt * 128:(kt + 1) * 128]:
                          tr(e, o, i_),
                          reads=[R_xs[j]] + S, writes=[rpb], signal=(kt == 7))
                tt(xT[:, :, tile * 128:(tile + 1) * 128], pbb[:, 0:1024].rearrange("p (k t) -> p k t", k=8),
                   gains[:, gcol0:gcol0 + 8].unsqueeze(2).to_broadcast([128, 8, 128]), ALU.mult,
                   [rpb] + S, [R_xT[kt][tile] for kt in range(8)])

            stages = [s0, s1, s2]

            def step(n):
                for k in (2, 1, 0):
                    t = n - base_lag - k
                    if 0 <= t < NTT:
                        stages[k](t)
            return step, NTT + base_lag + 2

        out_evs = []

        def rope_stages(blk_):
            b_, tk_ = blk_ // 2, (blk_ % 2) * TB
            CS = [R_cs]
            stages = []
            for sbk_ in range(2):
                c0_ = sbk_ * 512
                ang = ftmp[:, 2, :]
                rs_ = ftmp[:, 3, :]
                rc_ = ftmp[:, 0, :]

                def st_a(c0_=c0_, ang=ang):
                    posi = ftmp[:, 5, :].bitcast(I32)
                    fw.dma("sp", posi, pos[b_:b_ + 1, tk_ + c0_:tk_ + c0_ + 512].partition_broadcast(128),
                           writes=[R_ft[5]], slot="pos")
                    cp(ang, posi, [R_ft[5]], [R_ft[2]])
                    ts(ang, ang, invf, None, ALU.mult, None, [R_ft[2]] + S, [R_ft[2]])

                def st_b(ang=ang, rs_=rs_):
                    range_reduce(rs_, ang, 512, 0.0, [R_ft[2], R_ft[3]])

                def st_c(ang=ang, rc_=rc_):
                    range_reduce(rc_, ang, 512, math.pi / 2.0, [R_ft[2], R_ft[0]])

                def st_d(c0_=c0_, rs_=rs_, rc_=rc_):
                    act(sinT[:, c0_:c0_ + 512], rs_, AF.Sin, [R_ft[3]], CS)
                    act(cosT[:, c0_:c0_ + 512], rc_, AF.Sin, [R_ft[0]], CS)

                stages += [st_a, st_b, st_c, st_d]
            return stages

        def emit_rope_tables(blk_):
            for st_ in rope_stages(blk_):
                st_()

        for blk in range(4):
            b = blk // 2
            half = blk % 2
            tok0 = half * TB

            if half == 1:
                cp(kdT[:, :, 0:128], kdT[:, :, TB:TB + 128], [R_kd[2]], [R_kd[0]], e="act")
                cp(vtm[:, 0, :], vtm[:, NTT, :], [R_v[NTT]], [R_v[0]], e="act")
                cp(rk[:, 0, :], rk[:, NTT, :], [R_rk[NTT]], [R_rk[0]], e="act")

            def load_x(blk_, t, q="sp"):
                b_, tk_ = blk_ // 2, (blk_ % 2) * TB
                fw.dma(q, h[:, t, :], x[b_, tk_ + t * 128:tk_ + (t + 1) * 128, :], writes=[R_h[t]],
                       slot=("x%d" % t) if q == "sp" else ("xp%d" % t))

            if blk == 0:
                st_, tot_ = norm_driver(0, 0)
                for n_ in range(tot_):
                    st_(n_)
                emit_rope_tables(0)

            w_in0 = w_in[0]
            qslots = [load_chunk(chunk_cols(w_in0, o * 128)) for o in range(4)]
            kslots = []
            for kh in range(2):
                kslots.append(load_chunk([
                    (lambda r: r.rearrange("p (k c) -> p k c", k=8)[:, :, 0:64],
                     w_in0[:, 512 + kh * 64:512 + (kh + 1) * 64].rearrange("(k p) c -> p k c", p=128)),
                    (lambda r: r.rearrange("p (k c) -> p k c", k=8)[:, :, 64:128],
                     w_in0[:, 512 + kh * 64:512 + (kh + 1) * 64].rearrange("(k p) c -> p k c", p=128))]))
            vslot = load_chunk(chunk_cols(w_in0, 640))
            uslots = [load_chunk(chunk_cols(w_in0, 768 + o * 128)) for o in range(4)]

            def wv(slot, kt):
                return ring[:, slot, :].rearrange("p (k c) -> p k c", k=8)[:, kt, :]

            for sbk in range(2):
                c0 = sbk * 512
                tiles = list(range(4 * sbk, 4 * sbk + 4))
                xr = lambda kt: [R_xT[kt][t_] for t_ in tiles]
                CS = [R_cs]
                pk = rpk = None
                for oi in range(6):
                    slot = qslots[oi] if oi < 4 else kslots[oi - 4]
                    gcol = gains[:, 24:25] if oi < 4 else gains[:, 25:26]
                    pz, rz = bank()
                    for kt in range(8):
                        mm(pz[:, :], wv(slot, kt), xT[:, kt, c0:c0 + 512], kt == 0, kt == 7,
                           [R_ring[slot]] + xr(kt), [rz])
                    pq = oi % 2
                    sq, q1, rotb, ta, tb_ = [btmp[:, 5 * pq + n_, :] for n_ in range(5)]
                    r_sq, r_q1, r_rb, r_ta, r_tb = [R_bt[5 * pq + n_] for n_ in range(5)]
                    act(sq, pz[:, :], AF.Square, [rz], [r_sq])
                    act(q1, pz[:, :], AF.Copy, [rz] + S, [r_q1], scale=gcol)
                    prot, rrot = bank()
                    mm(prot[:, :], PiTb, q1, True, True, [r_q1] + S, [rrot])
                    act(rotb, prot[:, :], AF.Copy, [rrot], [r_rb])
                    tt(ta, q1, cosT[:, c0:c0 + 512], ALU.mult, [r_q1] + CS, [r_ta])
                    tt(tb_, rotb, sinT[:, c0:c0 + 512], ALU.mult, [r_rb] + CS, [r_tb])
                    if oi < 4:
                        pss, rss = bank()
                        mm(pss[:, :], BOb, sq, True, True, [r_sq] + S, [rss])
                        rstd = ftmp[:, pq, :]
                        r_rs = R_ft[pq]
                        act(rstd, pss[:, :], AF.Sqrt, [rss], [r_rs], scale=1.0 / 64.0, bias=EPS)
                        recip(rstd, rstd, [r_rs], [r_rs])
                        tt(ta, ta, tb_, ALU.add, [r_ta, r_tb], [r_ta])
                        tt(un[:, oi, c0:c0 + 512], ta, rstd, ALU.mult, [r_rs, r_ta], [R_un[oi][sbk]])
                    else:
                        kv = oi - 4
                        tt(kdT[:, kv, 128 + c0:128 + c0 + 512], ta, tb_, ALU.add, [r_ta, r_tb], [R_kd[1 + sbk]])
                        if pk is None:
                            pk, rpk = bank()
                        for ti in range(4):
                            mm(pk[:, ti * 2 + kv:ti * 2 + kv + 1], sq[:, ti * 128:(ti + 1) * 128], onesb[:, 0:1],
                               True, True, [r_sq] + S, [rpk], signal=(ti == 3))
                rks = small[:, 72:80]
                act(rks, pk[:, 0:8], AF.Sqrt, [rpk], [R_small], scale=1.0 / 128.0, bias=EPS)
                recip(rks, rks, [R_small], [R_small])
                ts(rk[:, 1 + 4 * sbk:5 + 4 * sbk, :], rks.rearrange("p (t k) -> p t k", k=2), 0.125, None, ALU.mult, None,
                   [R_small], [R_rk[1 + t_] for t_ in tiles])
                for s in range(4):
                    pz, rz = bank()
                    for kt in range(8):
                        mm(pz[:, :], wv(uslots[s], kt), xT[:, kt, c0:c0 + 512], kt == 0, kt == 7,
                           [R_ring[uslots[s]]] + xr(kt), [rz])
                    cp(un[:, 4 + s, c0:c0 + 512], pz[:, :], [rz], [R_un[4 + s][sbk]], e="act")
                pz, rz = bank()
                for ti, t_ in enumerate(tiles):
                    for kt in range(8):
                        mm(pz[:, ti * 128:(ti + 1) * 128], xT[:, kt, t_ * 128:(t_ + 1) * 128], wv(vslot, kt),
                           kt == 0, kt == 7, [R_ring[vslot], R_xT[kt][t_]], [rz], signal=(kt == 7 and ti == 3))
                cp(vtm[:, 1 + 4 * sbk:5 + 4 * sbk, :], pz[:, :].rearrange("p (a b) -> p a b", a=4), [rz],
                   [R_v[1 + t_] for t_ in tiles], e="act")

            def emit_P3():
                if half == 1:
                    cp(Vs[:, :, 0:1], Vs[:, :, 128:129], [R_Vs], [R_Vs])
                else:
                    memset(Vs[:, :, 0:1], 0.0, [R_Vs])
                for part in range(2):
                    for s in range(4):
                        for i in range(4):
                            pair = 4 * s + i
                            pz, rz = bank()
                            for j in range(8):
                                rhs = un[32 * i:32 * i + 32, 4 + s, :].rearrange("p (k j) -> p j k", j=8)[:, j, :]
                                mm(pz[:, 0:128], W_V[32 * i:32 * i + 32, s, part, j, :], rhs,
                                   j == 0, j == 7, [R_un[4 + s][0], R_un[4 + s][1]] + W, [rz], tp=(32 * i, 0))
                            st0 = part * 16 + pair
                            cp(Vs[:, st0, 1:129], pz[:, 0:128], [rz], [R_Vs], e="act")

            partC = None
            if blk == 0:
                chain, partC = ssm_precompute()
            else:
                emit_P3()

            V4 = Vs[:, :, 1:129].rearrange("p s (g j) -> p s g j", j=16)
            RSC = [R_Vs, R_ft[0], R_ft[1]]
            scan_items = []

            def scanA_step(j):
                prev = V4[:, :, :, j - 1]
                cur = V4[:, :, :, j]
                s0 = ftmp[:, 0, 0:256].rearrange("p (s g) -> p s g", g=8)
                s1 = ftmp[:, 1, 0:256].rearrange("p (s g) -> p s g", g=8)
                b8 = lambda a: a.unsqueeze(2).to_broadcast([128, a.shape[1], 8])
                tt(s0, prev, b8(scanA[:, 0, :]), ALU.mult, RSC + W, RSC)
                tt(s1[:, 0:16, :], V4[:, 16:32, :, j - 1], b8(scanA[:, 1, 0:16]), ALU.mult, RSC + W, RSC)
                tt(s1[:, 16:32, :], V4[:, 0:16, :, j - 1], b8(scanA[:, 1, 16:32]), ALU.mult, RSC + W, RSC)
                tt(cur, cur, s0, ALU.add, RSC, RSC)
                tt(cur, cur, s1, ALU.add, RSC, RSC)

            def scanB1():
                cp(Cc[:, :, 0], Vs[:, :, 0], RSC, RSC)
                t1 = ftmp[:, 0, 0:32]
                t2 = ftmp[:, 1, 0:32]
                for g in range(7):
                    tt(t1, Cc[:, :, g], TA[:, 0, :, 15], ALU.mult, RSC + W, RSC)
                    tt(t2[:, 0:16], Cc[:, 16:32, g], TA[:, 1, 0:16, 15], ALU.mult, RSC + W, RSC)
                    tt(t2[:, 16:32], Cc[:, 0:16, g], TA[:, 1, 16:32, 15], ALU.mult, RSC + W, RSC)
                    tt(t1, t1, t2, ALU.add, RSC, RSC)
                    tt(Cc[:, :, g + 1], V4[:, :, g, 15], t1, ALU.add, RSC, RSC)
                cp(Ccs[:, 0:16, :], Cc[:, 16:32, :], RSC, RSC + [R_ft[2]])
                cp(Ccs[:, 16:32, :], Cc[:, 0:16, :], RSC, RSC + [R_ft[2]])

            Ccs = ftmp[:, 2, 0:256].rearrange("p (s g) -> p s g", g=8)

            def scanB2(g):
                xg = V4[:, :, g, :]
                t1 = ftmp[:, 0, :].rearrange("p (s j) -> p s j", j=16)
                t2 = ftmp[:, 1, :].rearrange("p (s j) -> p s j", j=16)
                b16 = lambda a: a.unsqueeze(2).to_broadcast([128, a.shape[1], 16])
                tt(t1, TA[:, 0, :, :], b16(Cc[:, :, g]), ALU.mult, RSC + W, RSC)
                tt(t2, TA[:, 1, :, :], b16(Ccs[:, :, g]), ALU.mult, RSC + W + [R_ft[2]], RSC)
                tt(xg, xg, t1, ALU.add, RSC, RSC)
                tt(xg, xg, t2, ALU.add, RSC, RSC)

            for j in range(1, 16):
                scan_items.append(lambda j=j: scanA_step(j))
            scan_items.append(scanB1)
            for g in range(0 if half == 1 else 1, 8):
                scan_items.append(lambda g=g: scanB2(g))
            scan_sched = [[] for _ in range(16)]
            if blk == 0:
                for i_, rec_ in enumerate(chain):
                    scan_sched[(i_ * 16) // len(chain)].append(lambda r=rec_: fw.op(*r))
            else:
                for i_, it in enumerate(scan_items):
                    scan_sched[(i_ * 16) // len(scan_items)].append(it)

            gslots = []
            for f in range(4):
                gslots.append(load_chunk([(lambda r: r.rearrange("p (k c) -> p k c", k=8)[:, 0:4, :],
                                           glu_w[0][:, f * 128:(f + 1) * 128].rearrange("(k p) c -> p k c", p=128))]))
            oslots = [load_chunk(chunk_rows(w_out[0], kt)) for kt in range(8)]

            memset(btmp[64:128, 3, :], 0.0, [R_bt[3]])
            memset(btmp[0:64, 4, :], 0.0, [R_bt[4]])

            def chunk_blocks(c):
                gc = half * 16 + c
                blocks = []
                if c % 2 == 0:
                    if gc >= 2:
                        blocks.append((0, 128, c // 2, 64 * c, 2))
                    blocks.append((0, 64, c // 2 + 1, 64 * c + 128, 3))
                else:
                    if gc >= 3:
                        blocks.append((64, 128, (c - 1) // 2, 64 * c, 4))
                    blocks.append((0, 128, (c + 1) // 2, 64 * c + 64, 5))
                return blocks

            def emit_scores(c):
                qc0 = 64 * c
                sbk = c // 8
                pzh = [bank(), bank()]
                for bi, (lo, hi, vi, kc0, pbi) in enumerate(chunk_blocks(c)):
                    kres = set()
                    for cc in range(kc0, kc0 + (hi - lo), 64):
                        kres.add(R_kd[0] if cc < 128 else R_kd[1 + (cc - 128) // 512])
                    last = (bi == len(chunk_blocks(c)) - 1)
                    for kv in range(2):
                        for hf in range(2):
                            mm(pzh[hf][0][lo:hi, bi * 256 + 2 * kv * 64:bi * 256 + (2 * kv + 2) * 64],
                               kdT[hf * 64:(hf + 1) * 64, kv, kc0:kc0 + (hi - lo)],
                               un[hf * 64:(hf + 1) * 64, 2 * kv:2 * kv + 2, qc0:qc0 + 64], True, True,
                               list(kres) + [R_un[2 * kv][sbk], R_un[2 * kv + 1][sbk]], [pzh[hf][1]],
                               signal=(kv == 1 and last))
                return pzh

            def emit_rest(c, pzh):
                qc0 = 64 * c
                pss_ = []
                for bi, (lo, hi, vi, kc0, pbi) in enumerate(chunk_blocks(c)):
                    pbuf = btmp[:, pbi, :]
                    rpb_ = R_bt[pbi]
                    pv4 = pbuf.rearrange("p (f i q) -> p f i q", f=2, i=4)
                    for hf in range(2):
                        for kv in range(2):
                            act(pv4[lo:hi, hf, 2 * kv:2 * kv + 2, :],
                                pzh[hf][0][lo:hi, bi * 256 + 2 * kv * 64:bi * 256 + (2 * kv + 2) * 64]
                                .rearrange("p (i q) -> p i q", i=2), AF.Exp,
                                [pzh[hf][1], R_rk[vi]], [rpb_], scale=rk[lo:hi, vi, kv:kv + 1])
                    pss_.append((vi, pv4, rpb_))
                po, rpo = bank()
                nb = len(pss_)
                for kv in range(2):
                    for hf in range(2):
                        cs0 = 2 * kv * 64
                        for bi, (vi, pv4, rpb_) in enumerate(pss_):
                            mm(po[hf * 64:(hf + 1) * 64, cs0:cs0 + 128], vtm[:, vi, kv * 64:(kv + 1) * 64],
                               pv4[:, hf, 2 * kv:2 * kv + 2, :], bi == 0, bi == nb - 1, [R_v[vi], rpb_], [rpo],
                               signal=False)
                        for bi, (vi, pv4, rpb_) in enumerate(pss_):
                            mm(po[hf * 64:(hf + 1) * 64, 256 + cs0:256 + cs0 + 128], onesb[:, 0:64],
                               pv4[:, hf, 2 * kv:2 * kv + 2, :], bi == 0, False, [rpb_] + S, [rpo],
                               signal=False)
                        mm(po[hf * 64:(hf + 1) * 64, 256 + cs0:256 + cs0 + 128], onesb[hf * 64:hf * 64 + 1, 0:64],
                           esb[hf * 64:hf * 64 + 1, 2 * kv:2 * kv + 2, :], False, True, S, [rpo],
                           signal=(kv == 1 and hf == 1))
                dd = ftmp[:, 3 + (c % 2), 0:256]
                recip(dd, po[:, 256:512], [rpo], [R_ft[3 + (c % 2)]])
                tt(xT[:, 0:4, qc0:qc0 + 64], po[:, 0:256].rearrange("p (a b) -> p a b", a=4),
                   dd.rearrange("p (a b) -> p a b", a=4), ALU.mult, [rpo, R_ft[3 + (c % 2)]],
                   [R_xT[i][c // 2] for i in range(4)])

            pz_next = emit_scores(0)
            for c in range(16):
                for it in scan_sched[c]:
                    it()
                pz_cur = pz_next
                if c + 1 < 16:
                    pz_next = emit_scores(c + 1)
                emit_rest(c, pz_cur)

            if blk == 0:
                partC()
                emit_P3()
                for it in scan_items:
                    it()

            cp(Sb[:], Vs[:, :, 0:128], [R_Vs], [R_Sb])
            for s in range(4):
                for tg in range(2):
                    pz, rz = bank()
                    for t4 in range(4):
                        tp_ = tg * 4 + t4
                        o = pz[:, t4 * 128:(t4 + 1) * 128]
                        for j in range(tp_ + 1):
                            rhs = un[:, 4 + s, :].rearrange("p (k j) -> p j k", j=8)[:, j, :]
                            mm(o, W_K[:, s, tp_ - j, :], rhs, j == 0, False, [R_un[4 + s][0], R_un[4 + s][1]] + W, [rz],
                               signal=False)
                        n_ = 0
                        for i in range(4):
                            pair = 4 * s + i
                            for part in range(2):
                                n_ += 1
                                mm(pz[32 * i:32 * i + 32, t4 * 128:(t4 + 1) * 128], W_S[:, part, pair, tp_, :],
                                   Sb[:, part * 16 + pair, :], False, part == 1, [R_Sb] + W, [rz],
                                   signal=(n_ == 8 and t4 == 3), tp=(0, 32 * i))
                    ydst = un[:, s, :].rearrange("p (k j) -> p j k", j=8)[:, tg * 4:(tg + 1) * 4, :]
                    gp_ = (2 * s + tg) % 2
                    gw = ftmp[:, gp_, :]
                    rgw = R_ft[gp_]
                    act(gw, pz[:, :], AF.Square, [rz], [rgw])
                    ts(gw, gw, 0.044715, 1.0, ALU.mult, ALU.add, [rgw], [rgw])
                    tt(gw, gw, pz[:, :], ALU.mult, [rgw, rz], [rgw])
                    act(gw, gw, AF.Sigmoid, [rgw], [rgw], scale=2.0 * math.sqrt(2.0 / math.pi))
                    tt(ydst, gw.rearrange("p (a b) -> p a b", a=4), pz[:, :].rearrange("p (a b) -> p a b", a=4), ALU.mult,
                       [rgw, rz], [R_un[s][0], R_un[s][1]])

            for sbk in range(2):
                c0 = sbk * 512
                for f in range(4):
                    pz, rz = bank()
                    for kt in range(4):
                        mm(pz[:, :], wv(gslots[f], kt), un[:, kt, c0:c0 + 512], kt == 0, kt == 3,
                           [R_ring[gslots[f]], R_un[kt][sbk]], [rz])
                    sg = btmp[:, 0, :]
                    act(sg, pz[:, :], AF.Sigmoid, [rz] + S, [R_bt[0]], bias=gains[:, 34 + f:35 + f])
                    tt(xT[:, 4 + f, c0:c0 + 512], un[:, f, c0:c0 + 512], sg, ALU.mult, [R_un[f][sbk], R_bt[0]],
                       [R_xT[4 + f][t_] for t_ in range(4 * sbk, 4 * sbk + 4)])

            for kt in range(8):
                ts(ring[:, oslots[kt], :], ring[:, oslots[kt], :], gains[:, 26 + kt:27 + kt], None, ALU.mult, None,
                   [R_ring[oslots[kt]]] + S, [R_ring[oslots[kt]]])
            pssq, rssq = bank()
            for grp in range(2):
                for sbk in range(2):
                    c0 = sbk * 512
                    tl = list(range(4 * sbk, 4 * sbk + 4))
                    for f in range(4):
                        kt = grp * 4 + f
                        act(btmp[:, f, :], xT[:, kt, c0:c0 + 512], AF.Square, [R_xT[kt][t_] for t_ in tl], [R_bt[f]])
                    for ti, t_ in enumerate(tl):
                        col = grp * 8 + t_
                        for f in range(4):
                            mm(pssq[:, col:col + 1], btmp[:, f, ti * 128:(ti + 1) * 128], onesb[:, 0:1], f == 0, f == 3,
                               [R_bt[f]] + S, [rssq], signal=(f == 3 and ti == 3))
            rs16 = small[:, 40:56]
            act(rs16, pssq[:, 0:16], AF.Identity, [rssq], [R_scan], scale=1.0 / 512.0, bias=EPS)
            tt(rs16, rs16, mhalf.to_broadcast([128, 16]), ALU.pow, [R_scan] + S, [R_scan], e="pool")

            def tm_proj_add(slots, lhs_fn, lhs_res_fn, nk, after_tile=None):
                for t in range(NTT):
                    for hh in range(2):
                        pz, rz = bank()
                        for ki in range(nk):
                            mm(pz[:, :], lhs_fn(ki, t), ring[:, slots[ki], hh * 512:(hh + 1) * 512], ki == 0, ki == nk - 1,
                               [R_ring[slots[ki]]] + lhs_res_fn(ki, t), [rz])
                        tt(h[:, t, hh * 512:(hh + 1) * 512], h[:, t, hh * 512:(hh + 1) * 512], pz[:, :], ALU.add,
                           [rz, R_h[t]], [R_h[t]])
                    if after_tile is not None:
                        after_tile[0](t)
                if after_tile is not None:
                    for n_ in range(NTT, after_tile[1]):
                        after_tile[0](n_)

            n2step, n2tot = norm_driver(8, 0)
            for t in range(NTT):
                for hh in range(2):
                    hs = h[:, t, hh * 512:(hh + 1) * 512]
                    for grp in range(2):
                        pz, rz = bank()
                        for f in range(4):
                            ki = grp * 4 + f
                            mm(pz[:, :], xT[:, ki, t * 128:(t + 1) * 128], ring[:, oslots[ki], hh * 512:(hh + 1) * 512],
                               f == 0, f == 3, [R_ring[oslots[ki]], R_xT[ki][t]], [rz])
                        stt(hs, pz[:, :], rs16[:, grp * 8 + t:grp * 8 + t + 1], hs, ALU.mult, ALU.add,
                            [rz, R_h[t], R_scan], [R_h[t]])
                n2step(t)
            for n_ in range(NTT, n2tot):
                n2step(n_)

            for (f0, f1) in FFG:
                for f in range(f0, f1):
                    if blk < 3 and 2 <= f < 10:
                        if f == 2:
                            rstages = rope_stages(blk + 1)
                        rstages[f - 2]()
                    gsl = load_chunk(chunk_cols(w_gate[0], f * 128))
                    usl = load_chunk(chunk_cols(w_up[0], f * 128))
                    u_ = f - f0
                    for sbk in range(2):
                        c0 = sbk * 512
                        tl = list(range(4 * sbk, 4 * sbk + 4))
                        pg, rg = bank()
                        for kt in range(8):
                            mm(pg[:, :], wv(gsl, kt), xT[:, kt, c0:c0 + 512], kt == 0, kt == 7,
                               [R_ring[gsl]] + [R_xT[kt][t_] for t_ in tl], [rg])
                        pu, ru = bank()
                        for kt in range(8):
                            mm(pu[:, :], wv(usl, kt), xT[:, kt, c0:c0 + 512], kt == 0, kt == 7,
                               [R_ring[usl]] + [R_xT[kt][t_] for t_ in tl], [ru])
                        sg = btmp[:, sbk, :]
                        act(sg, pg[:, :], AF.Silu, [rg], [R_bt[sbk]])
                        tt(un[:, u_, c0:c0 + 512], pu[:, :], sg, ALU.mult, [ru, R_bt[sbk]], [R_un[u_][sbk]])
                dslots = [load_chunk(chunk_rows(w_down[0], f)) for f in range(f0, f1)]
                tm_proj_add(dslots, lambda ki, t: un[:, ki, t * 128:(t + 1) * 128], lambda ki, t: [R_un[ki][t // 4]],
                            f1 - f0, after_tile=norm_driver(16, 0) if f1 == NFF else None)

            for t in range(NTT):
                j = t % 2
                fw.dma("sp", pstage[:, j, :], p_in[b, tok0 + t * 128:tok0 + (t + 1) * 128, :], writes=[R_pst[j]],
                       slot="p%d" % j)
                cp(xs[:, j, 0:256], pstage[:, j, :], [R_pst[j]], [R_xs[j]], e="act")
                pb, rpb = bank()
                pbb = pb[:].bitcast(BF16)
                for kt in range(2):
                    fw.op("pe", lambda e, o=pbb[:, kt * 128:(kt + 1) * 128], i_=xs[:, j, kt * 128:(kt + 1) * 128]:
                          tr(e, o, i_),
                          reads=[R_xs[j]] + S, writes=[rpb], signal=(kt == 1))
                cp(pT[:, :, t * 128:(t + 1) * 128], pbb[:, 0:256].rearrange("p (k t) -> p k t", k=2), [rpb], [R_Sb])
            gsl = [load_chunk(chunk_rows(w_pg[0], kt)) for kt in range(8)]
            psl = [load_chunk(chunk_rows(w_pp[0], kt)) for kt in range(2)]
            n1step, n1tot = norm_driver(0, 2)
            for t in range(NTT):
                for hh in range(2):
                    cs_ = slice(hh * 512, (hh + 1) * 512)
                    pg, rg = bank()
                    for kt in range(8):
                        mm(pg[:, :], xT[:, kt, t * 128:(t + 1) * 128], ring[:, gsl[kt], cs_], kt == 0, False,
                           [R_ring[gsl[kt]], R_xT[kt][t]], [rg], signal=False)
                    mm(pg[:, :], onesb[0:1, :], bpleb[0:1, cs_], False, True, S + [R_bple], [rg])
                    pp, rp = bank()
                    for kt in range(2):
                        mm(pp[:, :], pT[:, kt, t * 128:(t + 1) * 128], ring[:, psl[kt], cs_], kt == 0, kt == 1,
                           [R_ring[psl[kt]], R_Sb], [rp])
                    sg = ftmp[:, hh, :]
                    act(sg, pg[:, :], AF.Sigmoid, [rg], [R_ft[hh]])
                    tt(sg, sg, pp[:, :], ALU.mult, [rp, R_ft[hh]], [R_ft[hh]])
                    tt(h[:, t, cs_], h[:, t, cs_], sg, ALU.add, [R_ft[hh], R_h[t]], [R_h[t]])
                ev = fw.dma("sp", out[b, tok0 + t * 128:tok0 + (t + 1) * 128, :], h[:, t, :], reads=[R_h[t]],
                            slot="o%d" % t)
                out_evs.append(ev)
                if blk < 3:
                    if t >= 1:
                        load_x(blk + 1, t - 1)
                    n1step(t)
            if blk < 3:
                load_x(blk + 1, NTT - 1)
                for n_ in range(NTT, n1tot):
                    n1step(n_)

        fw._wait("sp", out_evs)
    return nc


_PROG = {}


def kernel(**inputs):
    if "nc" not in _PROG:
        _PROG["nc"] = build_program()
    nc = _PROG["nc"]
    cst = make_consts()
    in_maps = []
    shared = {k: np.ascontiguousarray(v) for k, v in inputs.items() if k not in ("x", "p", "positions")}
    x = np.asarray(inputs["x"])
    p = np.asarray(inputs["p"])
    pos = np.asarray(inputs["positions"])
    for c in range(NCORE):
        m = dict(shared)
        m["x"] = np.ascontiguousarray(x[2 * c:2 * c + 2])
        m["p"] = np.ascontiguousarray(p[0, 2 * c:2 * c + 2])
        m["positions"] = np.ascontiguousarray(pos[2 * c:2 * c + 2])
        m["cst"] = cst
        in_maps.append(m)
    res = run_bass_kernel_spmd(nc, in_maps, core_ids=list(range(NCORE)))
    return np.concatenate([np.asarray(r["out"]) for r in res.results], axis=0).astype(np.float32)
```

```python
import math
import os
from contextlib import ExitStack

import numpy as np
import concourse.bass as bass
import concourse.mybir as mybir
from concourse.bass_utils import run_bass_kernel_spmd

F32 = mybir.dt.float32
BF16 = mybir.dt.bfloat16
I32 = mybir.dt.int32
AF = mybir.ActivationFunctionType
ALU = mybir.AluOpType

NCORE = 8
D = 1024
L = 2048
TB = 1024
NTT = TB // 128
DFF = 2816
NFF = DFF // 128
EPS = 1e-6
TWO_PI = 2.0 * math.pi
C1 = 6.28125
C2 = TWO_PI - C1
PI_CL = 3.1415925
NSLOT = 12
FFG = [(0, 8), (8, 16), (16, 22)]
SCAN_ENG = "pool"
LAG = 3
TR_MM = os.environ.get("K_TR_MM", "1") == "1"
QK_ENG = "dve"


class Res:
    __slots__ = ("w", "r")

    def __init__(self):
        self.w = None
        self.r = []


def RL(n):
    return [Res() for _ in range(n)]


class Fw:
    def __init__(self, nc, stack):
        self.nc = nc
        self.stack = stack
        self.eng = {"pe": nc.tensor, "act": nc.scalar, "dve": nc.vector, "pool": nc.gpsimd, "sp": nc.sync}
        self.sem = {k: stack.enter_context(nc.semaphore("s_" + k)) for k in self.eng}
        self.cnt = {k: 0 for k in self.eng}
        self.seen = {k: {} for k in self.eng}
        self.pend = {k: ([], []) for k in self.eng}
        self.dsem = {}
        self.dcnt = {}
        self.rec = None

    def _wait(self, e, deps):
        for ev in deps:
            if ev is None:
                continue
            sem, val, key = ev
            if key == "pe" and e == "pe":
                continue
            if self.seen[e].get(key, 0) >= val:
                continue
            self.seen[e][key] = val
            self.eng[e].wait_ge(sem, val)

    @staticmethod
    def _deps(reads, writes):
        d = []
        for r in reads:
            d.append(r.w)
        for w in writes:
            d.append(w.w)
            d.extend(w.r)
        return d

    def op(self, e, fn, reads=(), writes=(), signal=True):
        if self.rec is not None:
            self.rec.append((e, fn, list(reads), list(writes), signal))
            return None
        self._wait(e, self._deps(reads, writes))
        inst = fn(self.eng[e])
        pr, pw = self.pend[e]
        pr.extend(reads)
        pw.extend(writes)
        if signal:
            self.cnt[e] += 1
            inst.then_inc(self.sem[e], 1)
            ev = (self.sem[e], self.cnt[e], e)
            for r in pr:
                r.r.append(ev)
            for w in pw:
                w.w = ev
                w.r = []
            self.pend[e] = ([], [])
            return ev
        return None

    def dma(self, e, out, in_, reads=(), writes=(), slot="d0", **kw):
        if slot not in self.dsem:
            self.dsem[slot] = self.stack.enter_context(self.nc.semaphore("q_" + slot))
            self.dcnt[slot] = 0
        self._wait(e, self._deps(reads, writes))
        inst = self.eng[e].dma_start(out=out, in_=in_, **kw)
        self.dcnt[slot] += 16
        inst.then_inc(self.dsem[slot], 16)
        ev = (self.dsem[slot], self.dcnt[slot], "dma_" + slot)
        for r in reads:
            r.r.append(ev)
        for w in writes:
            w.w = ev
            w.r = []
        return ev


def make_consts():
    c = np.zeros((128, 648), np.float32)
    c[:, 0:128] = np.eye(128, dtype=np.float32)
    c[:, 128:256] = 1.0
    for hf in range(2):
        c[hf * 64:(hf + 1) * 64, 256 + hf * 64:256 + (hf + 1) * 64] = 1.0
    for hf in range(2):
        o = hf * 64
        for d in range(8):
            c[o + d + 8, 384 + o + d] = -1.0
            c[o + d, 384 + o + d + 8] = 1.0
    for i in range(4):
        c[i * 32:(i + 1) * 32, 512 + i * 32:512 + (i + 1) * 32] = 1.0
    for p in range(128):
        d = p % 64
        c[p, 640] = (500000.0 ** (-(d % 8) * (2.0 / 16.0))) if d < 16 else 0.0
        c[p, 641] = -0.5
    return c


def build_program():
    nc = bass.Bass("TRN2", target_bir_lowering=False)
    dram = lambda n, s, dt, k="ExternalInput": nc.dram_tensor(n, s, dt, kind=k).ap()
    x = dram("x", [2, L, D], F32)
    p_in = dram("p", [2, L, 256], F32)
    pos = dram("positions", [2, L], I32)
    cst = dram("cst", [128, 648], F32)
    norm1_g = dram("norm1_g", [1, D], F32)
    w_in = dram("w_in", [1, D, 1280], F32)
    q_norm_g = dram("q_norm_g", [1, 64], F32)
    k_norm_g = dram("k_norm_g", [1, 64], F32)
    attn_sinks = dram("attn_sinks", [1, 8], F32)
    lam_re = dram("ssm_lambda_re", [1, 32, 64], F32)
    lam_im = dram("ssm_lambda_im", [1, 32, 64], F32)
    b_re = dram("ssm_b_re", [1, 32, 64, 16], F32)
    b_im = dram("ssm_b_im", [1, 32, 64, 16], F32)
    c_re = dram("ssm_c_re", [1, 32, 16, 64], F32)
    c_im = dram("ssm_c_im", [1, 32, 16, 64], F32)
    ssm_d = dram("ssm_d", [1, 32, 16], F32)
    log_dt = dram("ssm_log_dt", [1, 32], F32)
    glu_w = dram("glu_w", [1, 512, 512], F32)
    glu_b = dram("glu_b", [1, 512], F32)
    ao_g = dram("attn_out_norm_g", [1, 512], F32)
    so_g = dram("ssm_out_norm_g", [1, 512], F32)
    w_out = dram("w_out", [1, D, D], F32)
    norm2_g = dram("norm2_g", [1, D], F32)
    w_gate = dram("w_ffn_gate", [1, D, DFF], F32)
    w_up = dram("w_ffn_up", [1, D, DFF], F32)
    w_down = dram("w_ffn_down", [1, DFF, D], F32)
    ple_g = dram("ple_norm_g", [1, D], F32)
    w_pg = dram("w_ple_gate", [1, D, D], F32)
    b_pg = dram("b_ple_gate", [1, D], F32)
    w_pp = dram("w_ple_proj", [1, 256, D], F32)
    out = dram("out", [2, L, D], F32, "ExternalOutput")

    with ExitStack() as st:
        fw = Fw(nc, st)
        st.enter_context(nc.allow_non_contiguous_dma(reason="small one-time parameter layout loads"))
        st.enter_context(nc.allow_low_precision(reason="bf16 matmul operands, fp32 accumulation"))
        sb = lambda n, s, dt: st.enter_context(nc.sbuf_tensor(n, s, dt))
        h = sb("h", [128, NTT, D], F32)
        xT = sb("xT", [128, 8, TB], BF16)
        un = sb("un", [128, 8, TB], BF16)
        kdT = sb("kdT", [128, 2, 128 + TB], BF16)
        vtm = sb("vtm", [128, NTT + 1, 128], BF16)
        Vs = sb("Vs", [128, 32, 129], F32)
        Sb = sb("Sb", [128, 32, 128], BF16)
        cosT = sb("cosT", [128, TB], BF16)
        sinT = sb("sinT", [128, TB], BF16)
        rk = sb("rk", [128, NTT + 1, 2], F32)
        ring = sb("ring", [128, NSLOT, 1024], BF16)
        W_V = sb("W_V", [128, 4, 2, 8, 128], BF16)
        W_K = sb("W_K", [128, 4, 8, 128], BF16)
        W_S = sb("W_S", [128, 2, 16, 8, 32], BF16)
        cstf = sb("cstf", [128, 648], F32)
        cstb = sb("cstb", [128, 640], BF16)
        xs = sb("xs", [128, 2, 1024], BF16)
        ftmp = sb("ftmp", [128, 6, 512], F32)
        btmp = sb("btmp", [128, 10, 512], BF16)
        small = sb("small", [128, 80], F32)
        gains = sb("gains", [128, 48], F32)
        bpleb = sb("bpleb", [1, 1024], BF16)
        pstage = sb("pstage", [128, 2, 256], F32)
        psb = [st.enter_context(nc.psum_tensor("ps%d" % i, [128, 512], F32)) for i in range(8)]

        identb = cstb[:, 0:128]
        onesb = cstb[:, 128:256]
        BOb = cstb[:, 256:384]
        PiTb = cstb[:, 384:512]
        identf = cstf[:, 0:128]
        BDf = cstf[:, 512:640]
        invf = cstf[:, 640:641]
        mhalf = cstf[:, 641:642]
        pT = Sb[:].rearrange("p a b -> p (a b)")[:, 0:2048].rearrange("p (a b) -> p a b", a=2)

        R_h = RL(NTT)
        R_xT = [RL(NTT) for _ in range(8)]
        R_un = [RL(2) for _ in range(8)]
        R_kd = RL(3)
        R_v = RL(NTT + 1)
        R_Vs = RL(1)[0]
        R_Sb = RL(1)[0]
        R_cs = RL(1)[0]
        R_ring = RL(NSLOT)
        R_ps = RL(8)
        R_xs = RL(2)
        R_sqs = RL(1)[0]
        R_ft = RL(6)
        R_bt = RL(10)
        R_rk = RL(NTT + 1)
        R_small = RL(1)[0]
        R_scan = RL(1)[0]
        R_pst = RL(2)
        R_setup = RL(1)[0]
        R_ssmw = RL(1)[0]
        R_out = RL(1)[0]

        st.enter_context(nc.Block())

        bank_ctr = [0]

        def bank():
            i = bank_ctr[0] % 8
            bank_ctr[0] += 1
            return psb[i], R_ps[i]

        def tr(e, o, i_):
            if TR_MM:
                return e.matmul(o, lhsT=i_, rhs=identb, start=True, stop=True, is_transpose=True)
            return e.transpose(out=o, in_=i_, identity=identb)

        def mm(out, lhsT, rhs, start, stop, reads, writes, signal=None, tp=None):
            kw = {}
            if tp is not None:
                kw["tile_position"] = tp
            return fw.op("pe", lambda e: e.matmul(out, lhsT=lhsT, rhs=rhs, start=start, stop=stop, **kw),
                         reads=reads, writes=writes, signal=(stop if signal is None else signal))

        def act(out, in_, func, reads, writes, scale=1.0, bias=0.0, accum_out=None):
            kw = {}
            if accum_out is not None:
                kw["accum_out"] = accum_out
            return fw.op("act", lambda e: e.activation(out=out, in_=in_, func=func, scale=scale, bias=bias, **kw),
                         reads=reads, writes=writes)

        def tt(out, in0, in1, op, reads, writes, e="dve"):
            return fw.op(e, lambda g: g.tensor_tensor(out=out, in0=in0, in1=in1, op=op), reads=reads, writes=writes)

        def ts(out, in0, s1, s2, op0, op1, reads, writes, e="dve"):
            if op1 is None:
                return fw.op(e, lambda g: g.tensor_scalar(out=out, in0=in0, scalar1=s1, scalar2=None, op0=op0),
                             reads=reads, writes=writes)
            return fw.op(e, lambda g: g.tensor_scalar(out=out, in0=in0, scalar1=s1, scalar2=s2, op0=op0, op1=op1),
                         reads=reads, writes=writes)

        def stt(out, in0, scalar, in1, op0, op1, reads, writes):
            return fw.op("dve", lambda g: g.scalar_tensor_tensor(out=out, in0=in0, scalar=scalar, in1=in1, op0=op0, op1=op1),
                         reads=reads, writes=writes)

        def cp(out, in_, reads, writes, e="dve"):
            if e == "act":
                return fw.op("act", lambda g: g.copy(out=out, in_=in_), reads=reads, writes=writes)
            return fw.op(e, lambda g: g.tensor_copy(out=out, in_=in_), reads=reads, writes=writes)

        def memset(ap, val, writes, e="dve"):
            return fw.op(e, lambda g: g.memset(ap, val), writes=writes)

        def recip(out, in_, reads, writes):
            return fw.op("dve", lambda g: g.reciprocal(out=out, in_=in_), reads=reads, writes=writes)

        sslot = ["setup"]

        def sdma(out_, in_):
            fw.dma("sp", out_, in_, slot=sslot[0])

        sdma(cstf[:], cst)
        sdma(gains[:, 0:8], norm1_g[0].rearrange("(k p) -> p k", p=128))
        sdma(gains[:, 8:16], norm2_g[0].rearrange("(k p) -> p k", p=128))
        sdma(gains[:, 16:24], ple_g[0].rearrange("(k p) -> p k", p=128))
        for hf in range(2):
            sdma(gains[hf * 64:(hf + 1) * 64, 24:25], q_norm_g[0].rearrange("(d o) -> d o", o=1))
            sdma(gains[hf * 64:(hf + 1) * 64, 25:26], k_norm_g[0].rearrange("(d o) -> d o", o=1))
            sdma(gains[hf * 64:(hf + 1) * 64, 38:42],
                 attn_sinks[0:1, :].rearrange("o (i f) -> o f i", f=2)[:, hf, :].partition_broadcast(64))
        sdma(gains[:, 26:30], ao_g[0].rearrange("(k p) -> p k", p=128))
        sdma(gains[:, 30:34], so_g[0].rearrange("(k p) -> p k", p=128))
        sdma(gains[:, 34:38], glu_b[0].rearrange("(k p) -> p k", p=128))
        sdma(gains[:, 42:46], ssm_d[0].rearrange("(s g) c -> (g c) s", s=4))
        R_bple = RL(1)[0]
        bplef = ftmp[0:1, 0:2, :].rearrange("p a b -> p (a b)")
        sdma(bplef, b_pg[0:1, :])
        ev_setup = (fw.dsem["setup"], fw.dcnt["setup"], "dma_setup")
        R_setup.w = ev_setup
        for t in range(NTT):
            fw.dma("sp", h[:, t, :], x[0, t * 128:(t + 1) * 128, :], writes=[R_h[t]], slot="x%d" % t)
        sslot[0] = "setup2"
        prm = sb("prm", [128, 3, 16], F32)
        sdma(prm[:, 0, :], lam_re[0].rearrange("(pr gp) n -> (gp n) pr", gp=2))
        sdma(prm[:, 1, :], lam_im[0].rearrange("(pr gp) n -> (gp n) pr", gp=2))
        for gp in range(2):
            sdma(prm[gp * 64:(gp + 1) * 64, 2, :],
                 log_dt[0:1, :].rearrange("o (pr gp) -> o gp pr", gp=2)[:, gp, :].partition_broadcast(64))
        v4 = lambda a: a.rearrange("p (a b c) -> p a b c", a=2, b=16)
        Sbf = Sb[:].rearrange("p a b -> p (a b)").bitcast(F32)
        Bm = v4(Sbf[:, 0:1024])
        Cm = v4(Sbf[:, 1024:2048])
        W_Sf = W_S[:].rearrange("p a b c d -> p (a b c d)").bitcast(F32)
        W_Kf = W_K[:].rearrange("p a b c -> p (a b c)").bitcast(F32)
        Xm = W_Sf[:, 0:1024].rearrange("p (a s n) -> p a s n", a=2, s=4)
        memset(Bm, 0.0, [R_ssmw])
        memset(W_Sf[:, 0:1024], 0.0, [R_ssmw])
        ev_ms = R_ssmw.w
        fw._wait("sp", [ev_ms])
        for part, (bsrc, csrc) in enumerate([(b_re, c_re), (b_im, c_im)]):
            bs = bsrc[0].rearrange("(pr gp) n c -> gp n pr c", gp=2)
            for gp in range(2):
                sdma(Bm[gp * 64:(gp + 1) * 64, part, :, gp * 16:(gp + 1) * 16], bs[gp])
            for g in range(32):
                s_, i_, gp = g // 8, (g % 8) // 2, g % 2
                sdma(Xm[32 * i_ + 16 * gp:32 * i_ + 16 * gp + 16, part, s_, 64 * gp:64 * gp + 64], csrc[0, g])
        R_ssmw.w = (fw.dsem["setup2"], fw.dcnt["setup2"], "dma_setup2")
        S = [R_setup]

        cp(cstb[:], cstf[:, 0:640], S, [R_setup])
        cp(bpleb[:], bplef, S, [R_bple, R_ft[0], R_ft[1]], e="act")
        act(gains[:, 38:42], gains[:, 38:42], AF.Exp, S, [R_setup])
        es2 = gains[:, 38:42]
        esb = sb("esb", [128, 4, 64], BF16)
        cp(esb[:], es2.unsqueeze(2).to_broadcast([128, 4, 64]), S, [R_setup])

        def range_reduce(dst, src, n, shift, rs):
            t = ftmp[:, 4, 0:n]
            ki = ftmp[:, 5, 0:n].bitcast(I32)
            ts(t, src, 1.0, shift, ALU.mult, ALU.add, rs, [R_ft[4]])
            ts(dst, t, 1.0 / TWO_PI, None, ALU.mult, None, [R_ft[4]], rs)
            cp(ki, dst, rs, [R_ft[5]])
            cp(dst, ki, [R_ft[5]], rs)
            stt(t, dst, -C1, t, ALU.mult, ALU.add, rs, [R_ft[4]])
            stt(t, dst, -C2, t, ALU.mult, ALU.add, rs, [R_ft[4]])
            ts(dst, t, -PI_CL, PI_CL, ALU.max, ALU.min, [R_ft[4]], rs)

        W = [R_ssmw]
        ssm = sb("ssmt", [128, 8, 144], F32)
        scanA = sb("scanA", [128, 3, 32], F32)
        TA = sb("TA", [128, 2, 32, 16], F32)
        Cc = sb("Cc", [128, 32, 8], F32)

        def ssm_precompute():
            for part in range(2):
                pb, rpb = bank()
                for s_ in range(4):
                    fw.op("pe", lambda e, o=pb[:, s_ * 128:(s_ + 1) * 128], i_=Xm[:, part, s_, :]:
                          e.matmul(o, lhsT=i_, rhs=identf, start=True, stop=True, is_transpose=True),
                          reads=[R_ssmw] + S, writes=[rpb], signal=(s_ == 3))
                cp(Cm[:, part], pb[:, :].rearrange("p (a b) -> p a b", b=32), [rpb], [R_ssmw])
            fw.rec = []
            lr = prm[:, 0, :]
            li = prm[:, 1, :]
            ts(lr, lr, -1e-4, None, ALU.min, None, W, W)
            act(prm[:, 2, :], prm[:, 2, :], AF.Exp, W, W)
            dtv = prm[:, 2, :]
            misc = ssm[:, 7, :]
            tt(misc[:, 0:16], lr, dtv, ALU.mult, W, W)
            tt(misc[:, 16:32], li, dtv, ALU.mult, W, W)
            for tau in range(9):
                ts(ssm[:, 0, tau * 16:(tau + 1) * 16], misc[:, 0:16], float(tau), None, ALU.mult, None, W, W)
                ts(ssm[:, 1, tau * 16:(tau + 1) * 16], misc[:, 16:32], float(tau), None, ALU.mult, None, W, W)
            act(ssm[:, 2, :], ssm[:, 0, :], AF.Exp, W, W)
            range_reduce(ssm[:, 3, :], ssm[:, 1, :], 144, 0.0, W)
            range_reduce(ssm[:, 4, :], ssm[:, 1, :], 144, math.pi / 2.0, W)
            act(ssm[:, 3, :], ssm[:, 3, :], AF.Sin, W, W)
            act(ssm[:, 4, :], ssm[:, 4, :], AF.Sin, W, W)
            tt(ssm[:, 5, :], ssm[:, 2, :], ssm[:, 4, :], ALU.mult, W, W)
            tt(ssm[:, 6, :], ssm[:, 2, :], ssm[:, 3, :], ALU.mult, W, W)
            Are = lambda tau: ssm[:, 5, tau * 16:(tau + 1) * 16]
            Aim = lambda tau: ssm[:, 6, tau * 16:(tau + 1) * 16]
            nr = misc[:, 32:48]
            ni = Aim(1)
            den = misc[:, 48:64]
            t1 = misc[:, 64:80]
            t2 = misc[:, 80:96]
            fre = misc[:, 96:112]
            fim = misc[:, 112:128]
            ts(nr, Are(1), -1.0, None, ALU.add, None, W, W)
            tt(den, lr, lr, ALU.mult, W, W)
            tt(t1, li, li, ALU.mult, W, W)
            tt(den, den, t1, ALU.add, W, W)
            recip(den, den, W, W)
            tt(t1, nr, lr, ALU.mult, W, W)
            tt(t2, ni, li, ALU.mult, W, W)
            tt(t1, t1, t2, ALU.add, W, W)
            tt(fre, t1, den, ALU.mult, W, W)
            tt(t1, ni, lr, ALU.mult, W, W)
            tt(t2, nr, li, ALU.mult, W, W)
            tt(t1, t1, t2, ALU.subtract, W, W)
            tt(fim, t1, den, ALU.mult, W, W)
            cp(scanA[:, 0, 0:16], Are(8), W, W)
            cp(scanA[:, 0, 16:32], Are(8), W, W)
            ts(scanA[:, 1, 0:16], Aim(8), -1.0, None, ALU.mult, None, W, W)
            cp(scanA[:, 1, 16:32], Aim(8), W, W)

            pw = ssm[:, 7, 128:144]
            pr_prev, pi_prev = Are(8), Aim(8)
            for j in range(16):
                if j > 0:
                    nr_ = ssm[:, 0, (j % 2) * 32:(j % 2) * 32 + 16]
                    ni_ = ssm[:, 0, (j % 2) * 32 + 16:(j % 2) * 32 + 32]
                    tt(nr_, pr_prev, Are(8), ALU.mult, W, W)
                    tt(pw, pi_prev, Aim(8), ALU.mult, W, W)
                    tt(nr_, nr_, pw, ALU.subtract, W, W)
                    tt(ni_, pr_prev, Aim(8), ALU.mult, W, W)
                    tt(pw, pi_prev, Are(8), ALU.mult, W, W)
                    tt(ni_, ni_, pw, ALU.add, W, W)
                    pr_prev, pi_prev = nr_, ni_
                cp(TA[:, 0, 0:16, j], pr_prev, W, W)
                cp(TA[:, 0, 16:32, j], pr_prev, W, W)
                ts(TA[:, 1, 0:16, j], pi_prev, -1.0, None, ALU.mult, None, W, W)
                cp(TA[:, 1, 16:32, j], pi_prev, W, W)

            bc = lambda a: a.unsqueeze(2).to_broadcast([128, 16, 32])
            Bb = v4(W_Kf[:, 0:1024])
            T1 = ftmp[:, 0, :].rearrange("p (a b) -> p a b", a=16)
            T2 = ftmp[:, 1, :].rearrange("p (a b) -> p a b", a=16)
            WT = W + [R_ft[0], R_ft[1]]
            tt(T1, Bm[:, 0], bc(fre), ALU.mult, WT, WT)
            tt(T2, Bm[:, 1], bc(fim), ALU.mult, WT, WT)
            tt(Bb[:, 0], T1, T2, ALU.subtract, WT, WT)
            tt(T1, Bm[:, 1], bc(fre), ALU.mult, WT, WT)
            tt(T2, Bm[:, 0], bc(fim), ALU.mult, WT, WT)
            tt(Bb[:, 1], T1, T2, ALU.add, WT, WT)
            Z = Vs[:].rearrange("p a b -> p (a b)").bitcast(BF16)[:, 0:8192].rearrange("p (a b c d) -> p a b c d", a=2, b=8, c=16)
            for tau in range(8):
                tt(T1, Bb[:, 0], bc(Are(tau)), ALU.mult, WT, WT)
                tt(T2, Bb[:, 1], bc(Aim(tau)), ALU.mult, WT, WT)
                tt(Z[:, 0, tau], T1, T2, ALU.subtract, WT, WT)
                tt(T1, Bb[:, 1], bc(Are(tau)), ALU.mult, WT, WT)
                tt(T2, Bb[:, 0], bc(Aim(tau)), ALU.mult, WT, WT)
                tt(Z[:, 1, tau], T1, T2, ALU.add, WT, WT)
            for tp_ in range(8):
                a_r = bc(Are(tp_ + 1))
                a_i = bc(Aim(tp_ + 1))
                tt(T1, Cm[:, 0], a_r, ALU.mult, WT, WT)
                tt(T2, Cm[:, 1], a_i, ALU.mult, WT, WT)
                tt(W_S[:, 0, :, tp_, :], T1, T2, ALU.subtract, WT, WT)
                tt(T1, Cm[:, 0], a_i, ALU.mult, WT, WT)
                tt(T2, Cm[:, 1], a_r, ALU.mult, WT, WT)
                stt(W_S[:, 1, :, tp_, :], T1, -1.0, T2, ALU.mult, ALU.subtract, WT, WT)
            Cmb = v4(pstage[:].rearrange("p a b -> p (a b)").bitcast(BF16))
            cp(Cmb[:, 0], Cm[:, 0], W, W)
            ts(Cmb[:, 1], Cm[:, 1], -1.0, None, ALU.mult, None, W, W)
            chain = fw.rec
            fw.rec = None

            def partC():
                for s in range(4):
                    for part in range(2):
                        pb, rpb = bank()
                        pbb = pb[:].bitcast(BF16)
                        for j in range(8):
                            zin = Z[:, part, 7 - j, 4 * s:4 * s + 4, :].rearrange("p a b -> p (a b)")
                            fw.op("pe", lambda e, o=pbb[:, j * 128:(j + 1) * 128], zi=zin: tr(e, o, zi),
                                  reads=W + S, writes=[rpb], signal=(j == 7))
                        cp(W_V[:, s, part].rearrange("p a b -> p (a b)"), pbb[:, 0:1024], [rpb], W)
                for s in range(4):
                    for half in range(2):
                        pb, rpb = bank()
                        for t4 in range(4):
                            tau = half * 4 + t4
                            o = pb[:, t4 * 128:(t4 + 1) * 128]
                            mm(o, Z[:, 0, tau, 4 * s:4 * s + 4, :].rearrange("p a b -> p (a b)"),
                               Cmb[:, 0, 4 * s:4 * s + 4, :].rearrange("p a b -> p (a b)"), True, False, W, [rpb], signal=False)
                            mm(o, Z[:, 1, tau, 4 * s:4 * s + 4, :].rearrange("p a b -> p (a b)"),
                               Cmb[:, 1, 4 * s:4 * s + 4, :].rearrange("p a b -> p (a b)"), False, True, W, [rpb], signal=(t4 == 3))
                        for t4 in range(4):
                            tau = half * 4 + t4
                            o = pb[:, t4 * 128:(t4 + 1) * 128]
                            if tau == 0:
                                tt(ftmp[:, 0, 0:128], o, BDf, ALU.mult, [rpb] + S, [R_ft[0]])
                                stt(W_K[:, s, 0, :], identf, gains[:, 42 + s:43 + s], ftmp[:, 0, 0:128], ALU.mult, ALU.add,
                                    [R_ft[0]] + S, W)
                            else:
                                tt(W_K[:, s, tau, :], o, BDf, ALU.mult, [rpb] + S, W)

                fence = [(fw.sem[k], fw.cnt[k], k) for k in ("pe", "act", "dve") if fw.cnt[k] > 0]
                for r_ in [R_Sb, R_Vs, R_pst[0], R_pst[1]]:
                    r_.r.extend(fence)

            return chain, partC

        slot_ctr = [0]

        def load_chunk(srcs):
            i = slot_ctr[0] % NSLOT
            slot_ctr[0] += 1
            for dst, src in srcs:
                fw.dma("pool", dst(ring[:, i, :]), src, writes=[R_ring[i]], slot="ring%d" % i)
            return i

        def chunk_cols(wmat, c0, ncol=128):
            return [(lambda r: r.rearrange("p (k c) -> p k c", k=8)[:, :, 0:ncol] if ncol == 128 else
                     r.rearrange("p (k c) -> p k c", k=8)[:, :, 0:ncol],
                     wmat[:, c0:c0 + ncol].rearrange("(k p) c -> p k c", p=128))]

        def chunk_rows(wmat, kt):
            return [(lambda r: r, wmat[kt * 128:(kt + 1) * 128, :])]

        R_nsm = RL(4)
        junk = btmp[:, 8:10, :].rearrange("p a b -> p (a b)")

        def norm_driver(gcol0, base_lag):
            def s0(tile):
                j4 = tile % 4
                ss = small[:, 2 * j4:2 * j4 + 1]
                rs = small[:, 2 * j4 + 1:2 * j4 + 2]
                act(junk, h[:, tile, :], AF.Square, [R_h[tile]], [R_bt[8], R_bt[9], R_nsm[j4]], accum_out=ss)
                ts(rs, ss, 1.0 / D, EPS, ALU.mult, ALU.add, [R_nsm[j4]], [R_nsm[j4]], e="pool")
                tt(rs, rs, mhalf, ALU.pow, [R_nsm[j4]] + S, [R_nsm[j4]], e="pool")

            def s1(tile):
                j, j4 = tile % 2, tile % 4
                rs = small[:, 2 * j4 + 1:2 * j4 + 2]
                act(xs[:, j, :], h[:, tile, :], AF.Copy, [R_h[tile], R_nsm[j4]], [R_xs[j]], scale=rs)

            def s2(tile):
                j = tile % 2
                pb, rpb = bank()
                pbb = pb[:].bitcast(BF16)
                for kt in range(8):
                    fw.op("pe", lambda e, o=pbb[:, kt * 128:(kt + 1) * 128], i_=xs[:, j, kt * 128:(kt + 1) * 128]:
                          tr(e, o, i_),
                          reads=[R_xs[j]] + S, writes=[rpb], signal=(kt == 7))
                tt(xT[:, :, tile * 128:(tile + 1) * 128], pbb[:, 0:1024].rearrange("p (k t) -> p k t", k=8),
                   gains[:, gcol0:gcol0 + 8].unsqueeze(2).to_broadcast([128, 8, 128]), ALU.mult,
                   [rpb] + S, [R_xT[kt][tile] for kt in range(8)])

            stages = [s0, s1, s2]

            def step(n):
                for k in (2, 1, 0):
                    t = n - base_lag - k
                    if 0 <= t < NTT:
                        stages[k](t)
            return step, NTT + base_lag + 2

        out_evs = []

        def rope_stages(blk_):
            b_, tk_ = blk_ // 2, (blk_ % 2) * TB
            CS = [R_cs]
            stages = []
            for sbk_ in range(2):
                c0_ = sbk_ * 512
                ang = ftmp[:, 2, :]
                rs_ = ftmp[:, 3, :]
                rc_ = ftmp[:, 0, :]

                def st_a(c0_=c0_, ang=ang):
                    posi = ftmp[:, 5, :].bitcast(I32)
                    fw.dma("sp", posi, pos[b_:b_ + 1, tk_ + c0_:tk_ + c0_ + 512].partition_broadcast(128),
                           writes=[R_ft[5]], slot="pos")
                    cp(ang, posi, [R_ft[5]], [R_ft[2]])
                    ts(ang, ang, invf, None, ALU.mult, None, [R_ft[2]] + S, [R_ft[2]])

                def st_b(ang=ang, rs_=rs_):
                    range_reduce(rs_, ang, 512, 0.0, [R_ft[2], R_ft[3]])

                def st_c(ang=ang, rc_=rc_):
                    range_reduce(rc_, ang, 512, math.pi / 2.0, [R_ft[2], R_ft[0]])

                def st_d(c0_=c0_, rs_=rs_, rc_=rc_):
                    act(sinT[:, c0_:c0_ + 512], rs_, AF.Sin, [R_ft[3]], CS)
                    act(cosT[:, c0_:c0_ + 512], rc_, AF.Sin, [R_ft[0]], CS)

                stages += [st_a, st_b, st_c, st_d]
            return stages

        def emit_rope_tables(blk_):
            for st_ in rope_stages(blk_):
                st_()

        for blk in range(4):
            b = blk // 2
            half = blk % 2
            tok0 = half * TB

            if half == 1:
                cp(kdT[:, :, 0:128], kdT[:, :, TB:TB + 128], [R_kd[2]], [R_kd[0]], e="act")
                cp(vtm[:, 0, :], vtm[:, NTT, :], [R_v[NTT]], [R_v[0]], e="act")
                cp(rk[:, 0, :], rk[:, NTT, :], [R_rk[NTT]], [R_rk[0]], e="act")

            def load_x(blk_, t, q="sp"):
                b_, tk_ = blk_ // 2, (blk_ % 2) * TB
                fw.dma(q, h[:, t, :], x[b_, tk_ + t * 128:tk_ + (t + 1) * 128, :], writes=[R_h[t]],
                       slot=("x%d" % t) if q == "sp" else ("xp%d" % t))

            if blk == 0:
                st_, tot_ = norm_driver(0, 0)
                for n_ in range(tot_):
                    st_(n_)
                emit_rope_tables(0)

            w_in0 = w_in[0]
            qslots = [load_chunk(chunk_cols(w_in0, o * 128)) for o in range(4)]
            kslots = []
            for kh in range(2):
                kslots.append(load_chunk([
                    (lambda r: r.rearrange("p (k c) -> p k c", k=8)[:, :, 0:64],
                     w_in0[:, 512 + kh * 64:512 + (kh + 1) * 64].rearrange("(k p) c -> p k c", p=128)),
                    (lambda r: r.rearrange("p (k c) -> p k c", k=8)[:, :, 64:128],
                     w_in0[:, 512 + kh * 64:512 + (kh + 1) * 64].rearrange("(k p) c -> p k c", p=128))]))
            vslot = load_chunk(chunk_cols(w_in0, 640))
            uslots = [load_chunk(chunk_cols(w_in0, 768 + o * 128)) for o in range(4)]

            def wv(slot, kt):
                return ring[:, slot, :].rearrange("p (k c) -> p k c", k=8)[:, kt, :]

            for sbk in range(2):
                c0 = sbk * 512
                tiles = list(range(4 * sbk, 4 * sbk + 4))
                xr = lambda kt: [R_xT[kt][t_] for t_ in tiles]
                CS = [R_cs]
                pk = rpk = None
                for oi in range(6):
                    slot = qslots[oi] if oi < 4 else kslots[oi - 4]
                    gcol = gains[:, 24:25] if oi < 4 else gains[:, 25:26]
                    pz, rz = bank()
                    for kt in range(8):
                        mm(pz[:, :], wv(slot, kt), xT[:, kt, c0:c0 + 512], kt == 0, kt == 7,
                           [R_ring[slot]] + xr(kt), [rz])
                    pq = oi % 2
                    sq, q1, rotb, ta, tb_ = [btmp[:, 5 * pq + n_, :] for n_ in range(5)]
                    r_sq, r_q1, r_rb, r_ta, r_tb = [R_bt[5 * pq + n_] for n_ in range(5)]
                    act(sq, pz[:, :], AF.Square, [rz], [r_sq])
                    act(q1, pz[:, :], AF.Copy, [rz] + S, [r_q1], scale=gcol)
                    prot, rrot = bank()
                    mm(prot[:, :], PiTb, q1, True, True, [r_q1] + S, [rrot])
                    act(rotb, prot[:, :], AF.Copy, [rrot], [r_rb])
                    tt(ta, q1, cosT[:, c0:c0 + 512], ALU.mult, [r_q1] + CS, [r_ta])
                    tt(tb_, rotb, sinT[:, c0:c0 + 512], ALU.mult, [r_rb] + CS, [r_tb])
                    if oi < 4:
                        pss, rss = bank()
                        mm(pss[:, :], BOb, sq, True, True, [r_sq] + S, [rss])
                        rstd = ftmp[:, pq, :]
                        r_rs = R_ft[pq]
                        act(rstd, pss[:, :], AF.Sqrt, [rss], [r_rs], scale=1.0 / 64.0, bias=EPS)
                        recip(rstd, rstd, [r_rs], [r_rs])
                        tt(ta, ta, tb_, ALU.add, [r_ta, r_tb], [r_ta])
                        tt(un[:, oi, c0:c0 + 512], ta, rstd, ALU.mult, [r_rs, r_ta], [R_un[oi][sbk]])
                    else:
                        kv = oi - 4
                        tt(kdT[:, kv, 128 + c0:128 + c0 + 512], ta, tb_, ALU.add, [r_ta, r_tb], [R_kd[1 + sbk]])
                        if pk is None:
                            pk, rpk = bank()
                        for ti in range(4):
                            mm(pk[:, ti * 2 + kv:ti * 2 + kv + 1], sq[:, ti * 128:(ti + 1) * 128], onesb[:, 0:1],
                               True, True, [r_sq] + S, [rpk], signal=(ti == 3))
                rks = small[:, 72:80]
                act(rks, pk[:, 0:8], AF.Sqrt, [rpk], [R_small], scale=1.0 / 128.0, bias=EPS)
                recip(rks, rks, [R_small], [R_small])
                ts(rk[:, 1 + 4 * sbk:5 + 4 * sbk, :], rks.rearrange("p (t k) -> p t k", k=2), 0.125, None, ALU.mult, None,
                   [R_small], [R_rk[1 + t_] for t_ in tiles])
                for s in range(4):
                    pz, rz = bank()
                    for kt in range(8):
                        mm(pz[:, :], wv(uslots[s], kt), xT[:, kt, c0:c0 + 512], kt == 0, kt == 7,
                           [R_ring[uslots[s]]] + xr(kt), [rz])
                    cp(un[:, 4 + s, c0:c0 + 512], pz[:, :], [rz], [R_un[4 + s][sbk]], e="act")
                pz, rz = bank()
                for ti, t_ in enumerate(tiles):
                    for kt in range(8):
                        mm(pz[:, ti * 128:(ti + 1) * 128], xT[:, kt, t_ * 128:(t_ + 1) * 128], wv(vslot, kt),
                           kt == 0, kt == 7, [R_ring[vslot], R_xT[kt][t_]], [rz], signal=(kt == 7 and ti == 3))
                cp(vtm[:, 1 + 4 * sbk:5 + 4 * sbk, :], pz[:, :].rearrange("p (a b) -> p a b", a=4), [rz],
                   [R_v[1 + t_] for t_ in tiles], e="act")

            def emit_P3():
                if half == 1:
                    cp(Vs[:, :, 0:1], Vs[:, :, 128:129], [R_Vs], [R_Vs])
                else:
                    memset(Vs[:, :, 0:1], 0.0, [R_Vs])
                for part in range(2):
                    for s in range(4):
                        for i in range(4):
                            pair = 4 * s + i
                            pz, rz = bank()
                            for j in range(8):
                                rhs = un[32 * i:32 * i + 32, 4 + s, :].rearrange("p (k j) -> p j k", j=8)[:, j, :]
                                mm(pz[:, 0:128], W_V[32 * i:32 * i + 32, s, part, j, :], rhs,
                                   j == 0, j == 7, [R_un[4 + s][0], R_un[4 + s][1]] + W, [rz], tp=(32 * i, 0))
                            st0 = part * 16 + pair
                            cp(Vs[:, st0, 1:129], pz[:, 0:128], [rz], [R_Vs], e="act")

            partC = None
            if blk == 0:
                chain, partC = ssm_precompute()
            else:
                emit_P3()

            V4 = Vs[:, :, 1:129].rearrange("p s (g j) -> p s g j", j=16)
            RSC = [R_Vs, R_ft[0], R_ft[1]]
            scan_items = []

            def scanA_step(j):
                prev = V4[:, :, :, j - 1]
                cur = V4[:, :, :, j]
                s0 = ftmp[:, 0, 0:256].rearrange("p (s g) -> p s g", g=8)
                s1 = ftmp[:, 1, 0:256].rearrange("p (s g) -> p s g", g=8)
                b8 = lambda a: a.unsqueeze(2).to_broadcast([128, a.shape[1], 8])
                tt(s0, prev, b8(scanA[:, 0, :]), ALU.mult, RSC + W, RSC)
                tt(s1[:, 0:16, :], V4[:, 16:32, :, j - 1], b8(scanA[:, 1, 0:16]), ALU.mult, RSC + W, RSC)
                tt(s1[:, 16:32, :], V4[:, 0:16, :, j - 1], b8(scanA[:, 1, 16:32]), ALU.mult, RSC + W, RSC)
                tt(cur, cur, s0, ALU.add, RSC, RSC)
                tt(cur, cur, s1, ALU.add, RSC, RSC)

            def scanB1():
                cp(Cc[:, :, 0], Vs[:, :, 0], RSC, RSC)
                t1 = ftmp[:, 0, 0:32]
                t2 = ftmp[:, 1, 0:32]
                for g in range(7):
                    tt(t1, Cc[:, :, g], TA[:, 0, :, 15], ALU.mult, RSC + W, RSC)
                    tt(t2[:, 0:16], Cc[:, 16:32, g], TA[:, 1, 0:16, 15], ALU.mult, RSC + W, RSC)
                    tt(t2[:, 16:32], Cc[:, 0:16, g], TA[:, 1, 16:32, 15], ALU.mult, RSC + W, RSC)
                    tt(t1, t1, t2, ALU.add, RSC, RSC)
                    tt(Cc[:, :, g + 1], V4[:, :, g, 15], t1, ALU.add, RSC, RSC)
                cp(Ccs[:, 0:16, :], Cc[:, 16:32, :], RSC, RSC + [R_ft[2]])
                cp(Ccs[:, 16:32, :], Cc[:, 0:16, :], RSC, RSC + [R_ft[2]])

            Ccs = ftmp[:, 2, 0:256].rearrange("p (s g) -> p s g", g=8)

            def scanB2(g):
                xg = V4[:, :, g, :]
                t1 = ftmp[:, 0, :].rearrange("p (s j) -> p s j", j=16)
                t2 = ftmp[:, 1, :].rearrange("p (s j) -> p s j", j=16)
                b16 = lambda a: a.unsqueeze(2).to_broadcast([128, a.shape[1], 16])
                tt(t1, TA[:, 0, :, :], b16(Cc[:, :, g]), ALU.mult, RSC + W, RSC)
                tt(t2, TA[:, 1, :, :], b16(Ccs[:, :, g]), ALU.mult, RSC + W + [R_ft[2]], RSC)
                tt(xg, xg, t1, ALU.add, RSC, RSC)
                tt(xg, xg, t2, ALU.add, RSC, RSC)

            for j in range(1, 16):
                scan_items.append(lambda j=j: scanA_step(j))
            scan_items.append(scanB1)
            for g in range(0 if half == 1 else 1, 8):
                scan_items.append(lambda g=g: scanB2(g))
            scan_sched = [[] for _ in range(16)]
            if blk == 0:
                for i_, rec_ in enumerate(chain):
                    scan_sched[(i_ * 16) // len(chain)].append(lambda r=rec_: fw.op(*r))
            else:
                for i_, it in enumerate(scan_items):
                    scan_sched[(i_ * 13) // len(scan_items)].append(it)
                scan_sched[13].append(lambda: cp(Sb[:], Vs[:, :, 0:128], [R_Vs], [R_Sb]))

            gslots = []
            for f in range(4):
                gslots.append(load_chunk([(lambda r: r.rearrange("p (k c) -> p k c", k=8)[:, 0:4, :],
                                           glu_w[0][:, f * 128:(f + 1) * 128].rearrange("(k p) c -> p k c", p=128))]))
            oslots = [load_chunk(chunk_rows(w_out[0], kt)) for kt in range(8)]

            memset(btmp[64:128, 3, :], 0.0, [R_bt[3]])
            memset(btmp[0:64, 4, :], 0.0, [R_bt[4]])

            def chunk_blocks(c):
                gc = half * 16 + c
                blocks = []
                if c % 2 == 0:
                    if gc >= 2:
                        blocks.append((0, 128, c // 2, 64 * c, 2))
                    blocks.append((0, 64, c // 2 + 1, 64 * c + 128, 3))
                else:
                    if gc >= 3:
                        blocks.append((64, 128, (c - 1) // 2, 64 * c, 4))
                    blocks.append((0, 128, (c + 1) // 2, 64 * c + 64, 5))
                return blocks

            def emit_scores(c):
                qc0 = 64 * c
                sbk = c // 8
                pzh = [bank(), bank()]
                for bi, (lo, hi, vi, kc0, pbi) in enumerate(chunk_blocks(c)):
                    kres = set()
                    for cc in range(kc0, kc0 + (hi - lo), 64):
                        kres.add(R_kd[0] if cc < 128 else R_kd[1 + (cc - 128) // 512])
                    last = (bi == len(chunk_blocks(c)) - 1)
                    for kv in range(2):
                        for hf in range(2):
                            mm(pzh[hf][0][lo:hi, bi * 256 + 2 * kv * 64:bi * 256 + (2 * kv + 2) * 64],
                               kdT[hf * 64:(hf + 1) * 64, kv, kc0:kc0 + (hi - lo)],
                               un[hf * 64:(hf + 1) * 64, 2 * kv:2 * kv + 2, qc0:qc0 + 64], True, True,
                               list(kres) + [R_un[2 * kv][sbk], R_un[2 * kv + 1][sbk]], [pzh[hf][1]],
                               signal=(kv == 1 and last))
                return pzh

            def emit_rest(c, pzh):
                qc0 = 64 * c
                pss_ = []
                for bi, (lo, hi, vi, kc0, pbi) in enumerate(chunk_blocks(c)):
                    pbuf = btmp[:, pbi, :]
                    rpb_ = R_bt[pbi]
                    pv4 = pbuf.rearrange("p (f i q) -> p f i q", f=2, i=4)
                    for hf in range(2):
                        for kv in range(2):
                            act(pv4[lo:hi, hf, 2 * kv:2 * kv + 2, :],
                                pzh[hf][0][lo:hi, bi * 256 + 2 * kv * 64:bi * 256 + (2 * kv + 2) * 64]
                                .rearrange("p (i q) -> p i q", i=2), AF.Exp,
                                [pzh[hf][1], R_rk[vi]], [rpb_], scale=rk[lo:hi, vi, kv:kv + 1])
                    pss_.append((vi, pv4, rpb_))
                po, rpo = bank()
                nb = len(pss_)
                for kv in range(2):
                    for hf in range(2):
                        cs0 = 2 * kv * 64
                        for bi, (vi, pv4, rpb_) in enumerate(pss_):
                            mm(po[hf * 64:(hf + 1) * 64, cs0:cs0 + 128], vtm[:, vi, kv * 64:(kv + 1) * 64],
                               pv4[:, hf, 2 * kv:2 * kv + 2, :], bi == 0, bi == nb - 1, [R_v[vi], rpb_], [rpo],
                               signal=False)
                        for bi, (vi, pv4, rpb_) in enumerate(pss_):
                            mm(po[hf * 64:(hf + 1) * 64, 256 + cs0:256 + cs0 + 128], onesb[:, 0:64],
                               pv4[:, hf, 2 * kv:2 * kv + 2, :], bi == 0, False, [rpb_] + S, [rpo],
                               signal=False)
                        mm(po[hf * 64:(hf + 1) * 64, 256 + cs0:256 + cs0 + 128], onesb[hf * 64:hf * 64 + 1, 0:64],
                           esb[hf * 64:hf * 64 + 1, 2 * kv:2 * kv + 2, :], False, True, S, [rpo],
                           signal=(kv == 1 and hf == 1))
                dd = ftmp[:, 3 + (c % 2), 0:256]
                recip(dd, po[:, 256:512], [rpo], [R_ft[3 + (c % 2)]])
                tt(xT[:, 0:4, qc0:qc0 + 64], po[:, 0:256].rearrange("p (a b) -> p a b", a=4),
                   dd.rearrange("p (a b) -> p a b", a=4), ALU.mult, [rpo, R_ft[3 + (c % 2)]],
                   [R_xT[i][c // 2] for i in range(4)])

            pz_next = emit_scores(0)
            for c in range(16):
                for it in scan_sched[c]:
                    it()
                pz_cur = pz_next
                if c + 1 < 16:
                    pz_next = emit_scores(c + 1)
                emit_rest(c, pz_cur)

            if blk == 0:
                partC()
                emit_P3()
                for it in scan_items:
                    it()

            if blk == 0:
                cp(Sb[:], Vs[:, :, 0:128], [R_Vs], [R_Sb])
            for s in range(4):
                for tg in range(2):
                    pz, rz = bank()
                    for t4 in range(4):
                        tp_ = tg * 4 + t4
                        o = pz[:, t4 * 128:(t4 + 1) * 128]
                        for j in range(tp_ + 1):
                            rhs = un[:, 4 + s, :].rearrange("p (k j) -> p j k", j=8)[:, j, :]
                            mm(o, W_K[:, s, tp_ - j, :], rhs, j == 0, False, [R_un[4 + s][0], R_un[4 + s][1]] + W, [rz],
                               signal=False)
                        n_ = 0
                        for i in range(4):
                            pair = 4 * s + i
                            for part in range(2):
                                n_ += 1
                                mm(pz[32 * i:32 * i + 32, t4 * 128:(t4 + 1) * 128], W_S[:, part, pair, tp_, :],
                                   Sb[:, part * 16 + pair, :], False, part == 1, [R_Sb] + W, [rz],
                                   signal=(n_ == 8 and t4 == 3), tp=(0, 32 * i))
                    ydst = un[:, s, :].rearrange("p (k j) -> p j k", j=8)[:, tg * 4:(tg + 1) * 4, :]
                    gp_ = (2 * s + tg) % 2
                    gw = ftmp[:, gp_, :]
                    rgw = R_ft[gp_]
                    act(gw, pz[:, :], AF.Square, [rz], [rgw])
                    ts(gw, gw, 0.044715, 1.0, ALU.mult, ALU.add, [rgw], [rgw])
                    tt(gw, gw, pz[:, :], ALU.mult, [rgw, rz], [rgw])
                    act(gw, gw, AF.Sigmoid, [rgw], [rgw], scale=2.0 * math.sqrt(2.0 / math.pi))
                    tt(ydst, gw.rearrange("p (a b) -> p a b", a=4), pz[:, :].rearrange("p (a b) -> p a b", a=4), ALU.mult,
                       [rgw, rz], [R_un[s][0], R_un[s][1]])

            for sbk in range(2):
                c0 = sbk * 512
                for f in range(4):
                    pz, rz = bank()
                    for kt in range(4):
                        mm(pz[:, :], wv(gslots[f], kt), un[:, kt, c0:c0 + 512], kt == 0, kt == 3,
                           [R_ring[gslots[f]], R_un[kt][sbk]], [rz])
                    sg = btmp[:, 0, :]
                    act(sg, pz[:, :], AF.Sigmoid, [rz] + S, [R_bt[0]], bias=gains[:, 34 + f:35 + f])
                    tt(xT[:, 4 + f, c0:c0 + 512], un[:, f, c0:c0 + 512], sg, ALU.mult, [R_un[f][sbk], R_bt[0]],
                       [R_xT[4 + f][t_] for t_ in range(4 * sbk, 4 * sbk + 4)])

            for kt in range(8):
                ts(ring[:, oslots[kt], :], ring[:, oslots[kt], :], gains[:, 26 + kt:27 + kt], None, ALU.mult, None,
                   [R_ring[oslots[kt]]] + S, [R_ring[oslots[kt]]])
            pssq, rssq = bank()
            for grp in range(2):
                for sbk in range(2):
                    c0 = sbk * 512
                    tl = list(range(4 * sbk, 4 * sbk + 4))
                    for f in range(4):
                        kt = grp * 4 + f
                        act(btmp[:, f, :], xT[:, kt, c0:c0 + 512], AF.Square, [R_xT[kt][t_] for t_ in tl], [R_bt[f]])
                    for ti, t_ in enumerate(tl):
                        col = grp * 8 + t_
                        for f in range(4):
                            mm(pssq[:, col:col + 1], btmp[:, f, ti * 128:(ti + 1) * 128], onesb[:, 0:1], f == 0, f == 3,
                               [R_bt[f]] + S, [rssq], signal=(f == 3 and ti == 3))
            rs16 = small[:, 40:56]
            act(rs16, pssq[:, 0:16], AF.Identity, [rssq], [R_scan], scale=1.0 / 512.0, bias=EPS)
            tt(rs16, rs16, mhalf.to_broadcast([128, 16]), ALU.pow, [R_scan] + S, [R_scan], e="pool")

            def tm_proj_add(slots, lhs_fn, lhs_res_fn, nk, after_tile=None):
                for t in range(NTT):
                    for hh in range(2):
                        pz, rz = bank()
                        for ki in range(nk):
                            mm(pz[:, :], lhs_fn(ki, t), ring[:, slots[ki], hh * 512:(hh + 1) * 512], ki == 0, ki == nk - 1,
                               [R_ring[slots[ki]]] + lhs_res_fn(ki, t), [rz])
                        tt(h[:, t, hh * 512:(hh + 1) * 512], h[:, t, hh * 512:(hh + 1) * 512], pz[:, :], ALU.add,
                           [rz, R_h[t]], [R_h[t]])
                    if after_tile is not None:
                        after_tile[0](t)
                if after_tile is not None:
                    for n_ in range(NTT, after_tile[1]):
                        after_tile[0](n_)

            n2step, n2tot = norm_driver(8, 0)
            for t in range(NTT):
                for hh in range(2):
                    hs = h[:, t, hh * 512:(hh + 1) * 512]
                    for grp in range(2):
                        pz, rz = bank()
                        for f in range(4):
                            ki = grp * 4 + f
                            mm(pz[:, :], xT[:, ki, t * 128:(t + 1) * 128], ring[:, oslots[ki], hh * 512:(hh + 1) * 512],
                               f == 0, f == 3, [R_ring[oslots[ki]], R_xT[ki][t]], [rz])
                        stt(hs, pz[:, :], rs16[:, grp * 8 + t:grp * 8 + t + 1], hs, ALU.mult, ALU.add,
                            [rz, R_h[t], R_scan], [R_h[t]])
                n2step(t)
            for n_ in range(NTT, n2tot):
                n2step(n_)

            for (f0, f1) in FFG:
                for f in range(f0, f1):
                    if blk < 3 and 2 <= f < 10:
                        if f == 2:
                            rstages = rope_stages(blk + 1)
                        rstages[f - 2]()
                    gsl = load_chunk(chunk_cols(w_gate[0], f * 128))
                    usl = load_chunk(chunk_cols(w_up[0], f * 128))
                    u_ = f - f0
                    for sbk in range(2):
                        c0 = sbk * 512
                        tl = list(range(4 * sbk, 4 * sbk + 4))
                        pg, rg = bank()
                        for kt in range(8):
                            mm(pg[:, :], wv(gsl, kt), xT[:, kt, c0:c0 + 512], kt == 0, kt == 7,
                               [R_ring[gsl]] + [R_xT[kt][t_] for t_ in tl], [rg])
                        pu, ru = bank()
                        for kt in range(8):
                            mm(pu[:, :], wv(usl, kt), xT[:, kt, c0:c0 + 512], kt == 0, kt == 7,
                               [R_ring[usl]] + [R_xT[kt][t_] for t_ in tl], [ru])
                        sg = btmp[:, sbk, :]
                        act(sg, pg[:, :], AF.Silu, [rg], [R_bt[sbk]])
                        tt(un[:, u_, c0:c0 + 512], pu[:, :], sg, ALU.mult, [ru, R_bt[sbk]], [R_un[u_][sbk]])
                dslots = [load_chunk(chunk_rows(w_down[0], f)) for f in range(f0, f1)]
                tm_proj_add(dslots, lambda ki, t: un[:, ki, t * 128:(t + 1) * 128], lambda ki, t: [R_un[ki][t // 4]],
                            f1 - f0, after_tile=norm_driver(16, 0) if f1 == NFF else None)

            for t in range(NTT):
                j = t % 2
                fw.dma("sp", pstage[:, j, :], p_in[b, tok0 + t * 128:tok0 + (t + 1) * 128, :], writes=[R_pst[j]],
                       slot="p%d" % j)
                cp(xs[:, j, 0:256], pstage[:, j, :], [R_pst[j]], [R_xs[j]], e="act")
                pb, rpb = bank()
                pbb = pb[:].bitcast(BF16)
                for kt in range(2):
                    fw.op("pe", lambda e, o=pbb[:, kt * 128:(kt + 1) * 128], i_=xs[:, j, kt * 128:(kt + 1) * 128]:
                          tr(e, o, i_),
                          reads=[R_xs[j]] + S, writes=[rpb], signal=(kt == 1))
                cp(pT[:, :, t * 128:(t + 1) * 128], pbb[:, 0:256].rearrange("p (k t) -> p k t", k=2), [rpb], [R_Sb])
            gsl = [load_chunk(chunk_rows(w_pg[0], kt)) for kt in range(8)]
            psl = [load_chunk(chunk_rows(w_pp[0], kt)) for kt in range(2)]
            n1step, n1tot = norm_driver(0, 2)
            for t in range(NTT):
                for hh in range(2):
                    cs_ = slice(hh * 512, (hh + 1) * 512)
                    pg, rg = bank()
                    for kt in range(8):
                        mm(pg[:, :], xT[:, kt, t * 128:(t + 1) * 128], ring[:, gsl[kt], cs_], kt == 0, False,
                           [R_ring[gsl[kt]], R_xT[kt][t]], [rg], signal=False)
                    mm(pg[:, :], onesb[0:1, :], bpleb[0:1, cs_], False, True, S + [R_bple], [rg])
                    pp, rp = bank()
                    for kt in range(2):
                        mm(pp[:, :], pT[:, kt, t * 128:(t + 1) * 128], ring[:, psl[kt], cs_], kt == 0, kt == 1,
                           [R_ring[psl[kt]], R_Sb], [rp])
                    sg = ftmp[:, hh, :]
                    act(sg, pg[:, :], AF.Sigmoid, [rg], [R_ft[hh]])
                    tt(sg, sg, pp[:, :], ALU.mult, [rp, R_ft[hh]], [R_ft[hh]])
                    tt(h[:, t, cs_], h[:, t, cs_], sg, ALU.add, [R_ft[hh], R_h[t]], [R_h[t]])
                ev = fw.dma("sp", out[b, tok0 + t * 128:tok0 + (t + 1) * 128, :], h[:, t, :], reads=[R_h[t]],
                            slot="o%d" % t)
                out_evs.append(ev)
                if blk < 3:
                    if t >= 1:
                        load_x(blk + 1, t - 1)
                    n1step(t)
            if blk < 3:
                load_x(blk + 1, NTT - 1)
                for n_ in range(NTT, n1tot):
                    n1step(n_)

        fw._wait("sp", out_evs)
    return nc


_PROG = {}


def kernel(**inputs):
    if "nc" not in _PROG:
        _PROG["nc"] = build_program()
    nc = _PROG["nc"]
    cst = make_consts()
    in_maps = []
    shared = {k: np.ascontiguousarray(v) for k, v in inputs.items() if k not in ("x", "p", "positions")}
    x = np.asarray(inputs["x"])
    p = np.asarray(inputs["p"])
    pos = np.asarray(inputs["positions"])
    for c in range(NCORE):
        m = dict(shared)
        m["x"] = np.ascontiguousarray(x[2 * c:2 * c + 2])
        m["p"] = np.ascontiguousarray(p[0, 2 * c:2 * c + 2])
        m["positions"] = np.ascontiguousarray(pos[2 * c:2 * c + 2])
        m["cst"] = cst
        in_maps.append(m)
    res = run_bass_kernel_spmd(nc, in_maps, core_ids=list(range(NCORE)))
    return np.concatenate([np.asarray(r["out"]) for r in res.results], axis=0).astype(np.float32)
```
